# Optimizing a Trainium2 kernel written in Bass

```python
import math
import jax, jax.numpy as jnp
from jax import lax
import numpy as np

D_MODEL = 1024
BATCH = 8
SEQ = 2048
DEPTH = 1

D_MIX = D_MODEL
HG_WIDTH = D_MIX // 2
HG_HEAD_DIM = 128
HG_HEADS = HG_WIDTH // HG_HEAD_DIM
HG_CHUNK = 64
SW_WIDTH = D_MIX - HG_WIDTH
SW_HEAD_DIM = 64
SW_Q_HEADS = SW_WIDTH // SW_HEAD_DIM
SW_KV_HEADS = SW_Q_HEADS // 4
SW_KV_WIDTH = SW_KV_HEADS * SW_HEAD_DIM
WINDOW = 128
ROPE_THETA = 500000.0
ROPE_DIM = SW_HEAD_DIM // 4
DN_ALPHA = (2.0 * DEPTH) ** 0.25
DN_BETA = (8.0 * DEPTH) ** -0.25
LN_EPS = 1e-5
RMS_EPS = 1e-6
IN_SIZES = (HG_WIDTH, HG_WIDTH, HG_WIDTH, HG_WIDTH, SW_WIDTH, SW_KV_WIDTH, SW_KV_WIDTH, SW_WIDTH)
IN_WIDTH = sum(IN_SIZES)
IN_SPLITS = tuple(int(s) for s in np.cumsum(IN_SIZES)[:-1])

kernel_name = "hybrid_hgrn2_swa_sink_deepnorm"


def hgrn2_chunk(q, k, v, log_f):
    B, T, H, K = q.shape
    V = v.shape[-1]
    C = HG_CHUNK
    N = T // C
    q = q.reshape(B, N, C, H, K)
    k = k.reshape(B, N, C, H, K)
    v = v.reshape(B, N, C, H, V)
    log_f = log_f.reshape(B, N, C, H, K)
    G = jnp.cumsum(log_f, axis=2)
    G_last = G[:, :, -1]
    q_dec = q * jnp.exp(G)
    k_dec = k * jnp.exp(-G)
    causal = jnp.tril(jnp.ones((C, C), dtype=bool))
    A = jnp.einsum('bnthk,bnshk->bnhts', q_dec, k_dec)
    A = jnp.where(causal, A, 0.0)
    o_intra = jnp.einsum('bnhts,bnshv->bnthv', A, v)
    inc = jnp.einsum('bnshk,bnshv->bnhkv', k * jnp.exp(G_last[:, :, None] - G), v)
    decay = jnp.exp(G_last)

    def step(S, xs):
        d, u = xs
        return d[..., None] * S + u, S

    S0 = jnp.zeros((B, H, K, V), q.dtype)
    _, S_prev = lax.scan(step, S0, (jnp.moveaxis(decay, 1, 0), jnp.moveaxis(inc, 1, 0)))
    S_prev = jnp.moveaxis(S_prev, 0, 1)
    o_inter = jnp.einsum('bnthk,bnhkv->bnthv', q_dec, S_prev)
    return (o_intra + o_inter).reshape(B, T, H, V)


def partial_rope(x, cos, sin):
    half = ROPE_DIM // 2
    x1 = x[..., :half]
    x2 = x[..., half:ROPE_DIM]
    rot = jnp.concatenate([x1 * cos - x2 * sin, x2 * cos + x1 * sin], axis=-1)
    return jnp.concatenate([rot, x[..., ROPE_DIM:]], axis=-1)


def swa_with_sinks(q, k, v, sinks):
    B, T, Hq, D = q.shape
    Hkv = k.shape[2]
    G = Hq // Hkv
    W = WINDOW
    nb = T // W
    qb = q.reshape(B, nb, W, Hkv, G, D)
    pad = jnp.zeros((B, W, Hkv, D), k.dtype)
    kb = jnp.concatenate([pad, k], axis=1).reshape(B, nb + 1, W, Hkv, D)
    vb = jnp.concatenate([pad, v], axis=1).reshape(B, nb + 1, W, Hkv, D)
    kw = jnp.concatenate([kb[:, :-1], kb[:, 1:]], axis=2)
    vw = jnp.concatenate([vb[:, :-1], vb[:, 1:]], axis=2)
    s = jnp.einsum('bnqhgd,bnkhd->bnhgqk', qb, kw).astype(jnp.float32) * (D ** -0.5)
    t = jnp.arange(W)[:, None]
    j = jnp.arange(2 * W)[None, :]
    band = (j > t) & (j <= t + W)
    valid = (jnp.arange(nb)[:, None, None] > 0) | (j[None] >= W)
    mask = band[None] & valid
    s = jnp.where(mask[None, :, None, None], s, -jnp.inf)
    sink = sinks.astype(jnp.float32).reshape(1, 1, Hkv, G, 1, 1)
    m = jnp.maximum(jnp.max(s, axis=-1, keepdims=True), sink)
    p = jnp.exp(s - m)
    denom = jnp.sum(p, axis=-1, keepdims=True) + jnp.exp(sink - m)
    p = (p / denom).astype(v.dtype)
    o = jnp.einsum('bnhgqk,bnkhd->bnqhgd', p, vw)
    return o.reshape(B, T, Hq * D)


def setup_inputs(seed: int = 0) -> dict:
    key = jax.random.key(seed)
    ks = jax.random.split(key, 8)
    x = jax.random.normal(ks[0], (BATCH, SEQ, D_MODEL), jnp.float32)
    col_scale = jnp.concatenate([
        jnp.ones((HG_WIDTH,)), jnp.ones((HG_WIDTH,)), jnp.full((HG_WIDTH,), DN_BETA), jnp.ones((HG_WIDTH,)),
        jnp.ones((SW_WIDTH,)), jnp.ones((SW_KV_WIDTH,)), jnp.full((SW_KV_WIDTH,), DN_BETA), jnp.ones((SW_WIDTH,)),
    ]).astype(jnp.float32)
    w_in = jax.random.normal(ks[1], (DEPTH, D_MODEL, IN_WIDTH), jnp.float32) * (D_MODEL ** -0.5) * col_scale
    lb_logits = 0.1 * jax.random.normal(ks[2], (DEPTH + 1, HG_WIDTH), jnp.float32)
    hg_norm_w = 1.0 + 0.02 * jax.random.normal(ks[3], (DEPTH, HG_WIDTH), jnp.float32)
    sinks = 0.5 * jax.random.normal(ks[4], (DEPTH, SW_Q_HEADS), jnp.float32)
    w_out = jax.random.normal(ks[5], (DEPTH, D_MIX, D_MODEL), jnp.float32) * (D_MIX ** -0.5) * DN_BETA
    ln_g = 1.0 + 0.02 * jax.random.normal(ks[6], (DEPTH, D_MODEL), jnp.float32)
    ln_b = 0.02 * jax.random.normal(ks[7], (DEPTH, D_MODEL), jnp.float32)
    return {"x": x, "w_in": w_in, "lb_logits": lb_logits, "hg_norm_w": hg_norm_w,
            "sinks": sinks, "w_out": w_out, "ln_g": ln_g, "ln_b": ln_b}


def reference(x, w_in, lb_logits, hg_norm_w, sinks, w_out, ln_g, ln_b):
    B, T, _ = x.shape
    f32 = jnp.float32
    dt = x.dtype
    pos = jnp.arange(T, dtype=f32)
    inv_freq = ROPE_THETA ** (-jnp.arange(0, ROPE_DIM, 2, dtype=f32) / ROPE_DIM)
    ang = pos[:, None] * inv_freq[None, :]
    cos = jnp.cos(ang)[:, None, :].astype(dt)
    sin = jnp.sin(ang)[:, None, :].astype(dt)
    lower_bounds = jnp.cumsum(jax.nn.softmax(lb_logits.astype(f32), axis=0), axis=0)
    h_res = x
    for layer in range(DEPTH):
        h = jnp.einsum('btd,de->bte', h_res, w_in[layer])
        hq, hf, hi, hg, aq, ak, av, ag = jnp.split(h, IN_SPLITS, axis=-1)
        lb = lower_bounds[layer]
        f = lb + (1.0 - lb) * jax.nn.sigmoid(hf.astype(f32))
        log_f = jnp.log(f)
        k_in = 1.0 - f
        q_h = jax.nn.silu(hq.astype(f32))
        rs = lambda a: a.reshape(B, T, HG_HEADS, HG_HEAD_DIM)
        o_h = hgrn2_chunk(rs(q_h), rs(k_in), rs(hi.astype(f32)), rs(log_f))
        o_h = o_h * lax.rsqrt(jnp.mean(o_h * o_h, axis=-1, keepdims=True) + RMS_EPS)
        o_h = o_h * hg_norm_w[layer].astype(f32).reshape(HG_HEADS, HG_HEAD_DIM)
        o_h = o_h.reshape(B, T, HG_WIDTH).astype(dt) * jax.nn.silu(hg)
        q_a = partial_rope(aq.reshape(B, T, SW_Q_HEADS, SW_HEAD_DIM), cos, sin)
        k_a = partial_rope(ak.reshape(B, T, SW_KV_HEADS, SW_HEAD_DIM), cos, sin)
        v_a = av.reshape(B, T, SW_KV_HEADS, SW_HEAD_DIM)
        o_a = swa_with_sinks(q_a, k_a, v_a, sinks[layer]) * jax.nn.silu(ag)
        mix = jnp.concatenate([o_h, o_a], axis=-1)
        out = jnp.einsum('bte,ed->btd', mix, w_out[layer])
        z = DN_ALPHA * h_res.astype(f32) + out.astype(f32)
        mu = jnp.mean(z, axis=-1, keepdims=True)
        var = jnp.mean(jnp.square(z - mu), axis=-1, keepdims=True)
        z = (z - mu) * lax.rsqrt(var + LN_EPS)
        h_res = (z * ln_g[layer].astype(f32) + ln_b[layer].astype(f32)).astype(dt)
    return h_res
```

```python
import numpy as np
from contextlib import ExitStack
import concourse.bass as bass
import concourse.mybir as mybir
from concourse.bass_utils import run_bass_kernel_spmd

F32 = mybir.dt.float32
BF16 = mybir.dt.bfloat16
AF = mybir.ActivationFunctionType
ALU = mybir.AluOpType
AX = mybir.AxisListType

T = 2048
D = 1024
NTT = 16
DN_ALPHA = 2.0 ** 0.25
LN_EPS = 1e-5
RMS_EPS = 1e-6
LN2 = 0.6931471805599453

SAME_ENG_SYNC = True
NDMA = 24


class Sched:
    ENG = ('pe', 'act', 'dve', 'pool', 'sp')

    def __init__(self, nc, es):
        self.nc = nc
        self.e = {'pe': nc.tensor, 'act': nc.scalar, 'dve': nc.vector,
                  'pool': nc.gpsimd, 'sp': nc.sync}
        self.sem = {k: es.enter_context(nc.semaphore('sem_' + k)) for k in self.ENG}
        self.cnt = {k: 0 for k in self.ENG}
        self.seen = {k: {} for k in self.ENG}
        self.lastw = {}
        self.readers = {}
        self.dsem = [es.enter_context(nc.semaphore('dsem%d' % i)) for i in range(NDMA)]
        self.dcnt = [0] * NDMA
        self.dpool = {'pool': list(range(0, NDMA // 2)), 'sp': list(range(NDMA // 2, NDMA)),
                      'act': list(range(NDMA // 2, NDMA))}
        self.drr = {'pool': 0, 'sp': 0, 'act': 0}

    def _semobj(self, s):
        return self.dsem[s[1]] if isinstance(s, tuple) else self.sem[s]

    def _deps(self, E, reads, writes, extra=()):
        need = {}

        def add(dep):
            if dep is None:
                return
            s, c = dep
            if c > need.get(s, 0):
                need[s] = c
        for k in reads:
            add(self.lastw.get(k))
        for k in writes:
            add(self.lastw.get(k))
            for dep in self.readers.get(k, {}).items():
                add(dep)
        for dep in extra:
            add(dep)
        out = []
        for s, c in need.items():
            if s == E and (E == 'pe' or not SAME_ENG_SYNC):
                continue
            if c > self.seen[E].get(s, 0):
                self.seen[E][s] = c
                out.append((self._semobj(s), c))
        return out

    def _record(self, tag, reads, writes):
        s, c = tag
        for k in reads:
            d = self.readers.setdefault(k, {})
            if c > d.get(s, 0):
                d[s] = c
        for k in writes:
            self.lastw[k] = tag
            self.readers[k] = {}

    REGION_CFG = {0: ('prio', 0.18, 0.13, 0.18, 0.13, 0.0, 0, 0.0, 0.10, 0.215),
                  1: ('prio', 0.2, 0.15, 0.22, 0.16, 0.0, 0, 0.0, 0.12, 0.215)}

    def _cost(self, E, n):
        cfg = self.cfg
        if E == 'pe':
            big = cfg[9] if len(cfg) > 9 else 0.215
            small = cfg[8] if len(cfg) > 8 else 0.11
            return big if n >= 512 else small
        if E == 'act':
            return cfg[3] + n / 1200.0
        if E == 'dve':
            return cfg[4] + n / 960.0
        if E == 'pool':
            return 0.2 + n * 0.0027
        return 0.1

    SLACK = 0.0
    SYNC_LAT = 0.15
    ACT_FIX = 0.28
    DVE_FIX = 0.16
    CONFIGS = (('prio', 0.0), ('prio', 0.2), ('prio', 0.4), ('prio', 0.6), ('prio', 0.8), ('prio', 1.0), ('prio', 1.3), ('old', 0.5))

    def begin(self):
        self.rec = []
        self.cur = {}
        self.region = getattr(self, 'region', -1) + 1
        self.cfg = self.REGION_CFG[self.region]

    def op(self, E, fn, reads=(), writes=(), inc=True, n=None, cost=None):
        if getattr(self, 'rec', None) is not None:
            if n is None:
                n = 128 if E == 'pe' else 512
            c = cost if cost is not None else self._cost(E, n)
            u = self.cur.get(E)
            if u is None:
                u = dict(E=E, items=[], reads=[], writes=[], cost=0.0)
                self.cur[E] = u
            u['items'].append(('op', E, fn, tuple(reads), tuple(writes), inc))
            u['reads'] += list(reads)
            u['writes'] += list(writes)
            u['cost'] += c
            if inc:
                u['lat'] = u['cost']
                self.rec.append(u)
                self.cur[E] = None
            return None
        return self._emit_op(E, fn, reads, writes, inc)

    def dma(self, Q, out, in_, reads=(), writes=(), nbytes=65536, **kw):
        if getattr(self, 'rec', None) is not None:
            issue = 1.1 if Q == 'pool' else 0.15
            u = dict(E=Q, items=[('dma', Q, out, in_, tuple(reads), tuple(writes), kw)], reads=list(reads),
                     writes=list(writes), cost=issue, lat=issue + 2.0 + nbytes / 150e3)
            assert self.cur.get(Q) is None
            self.rec.append(u)
            return None
        return self._emit_dma(Q, out, in_, reads, writes, **kw)

    def end(self):
        units = self.rec
        self.rec = None
        assert all(v is None for v in self.cur.values())
        N = len(units)
        deps = [set() for _ in range(N)]
        lastw, readers = {}, {}
        for j, u in enumerate(units):
            for k in u['reads']:
                if k in lastw:
                    deps[j].add(lastw[k])
            for k in u['writes']:
                if k in lastw:
                    deps[j].add(lastw[k])
                deps[j].update(readers.get(k, ()))
            for k in u['reads']:
                readers.setdefault(k, []).append(j)
            for k in u['writes']:
                lastw[k] = j
                readers[k] = []
            deps[j].discard(j)
        succ = [[] for _ in range(N)]
        for j in range(N):
            for d in deps[j]:
                succ[d].append(j)
        prio = [0.0] * N
        for j in range(N - 1, -1, -1):
            prio[j] = units[j]['lat'] + max([prio[s] for s in succ[j]], default=0.0)
        SYNC = self.cfg[2]
        PE_MARGIN = self.cfg[7] if len(self.cfg) > 7 else 0.0
        if self.cfg[5] > 0:
            rng = np.random.RandomState(self.cfg[6])
            prio = [p * (1.0 + self.cfg[5] * rng.rand()) for p in prio]

        def simulate(mode, slack):
            indeg = [len(d) for d in deps]
            ready = [j for j in range(N) if indeg[j] == 0]
            eng_free = {}
            avail = [0.0] * N
            order = []
            while ready:
                cands = []
                for j in ready:
                    u = units[j]
                    est = eng_free.get(u['E'], 0.0)
                    for d in deps[j]:
                        t = avail[d] + (SYNC if units[d]['E'] != u['E'] else 0.05)
                        if u['E'] == 'pe' and units[d]['E'] != 'pe':
                            t += PE_MARGIN
                        if t > est:
                            est = t
                    cands.append((est, j))
                mn = min(c[0] for c in cands)
                win = [c for c in cands if c[0] <= mn + slack]
                if mode == 'old':
                    est, j = min(win, key=lambda c: (c[1], c[0]))
                else:
                    est, j = min(win, key=lambda c: (-prio[c[1]], c[0], c[1])) if slack > 0 else \
                        min(win, key=lambda c: (c[0], -prio[c[1]], c[1]))
                ready.remove(j)
                u = units[j]
                eng_free[u['E']] = est + u['cost']
                avail[j] = est + u['lat']
                order.append((est, len(order), j))
                for s in succ[j]:
                    indeg[s] -= 1
                    if indeg[s] == 0:
                        ready.append(s)
            return (max(avail) if N else 0.0), order

        best = None
        for mode, slack in (self.cfg[0:2],):
            span, order = simulate(mode, slack)
            if best is None or span < best[0]:
                best = (span, order, mode, slack)
        self.sim_span, order = best[0], best[1]
        self.sim_choice = best[2:]
        assert len(order) == N
        for _, _, j in sorted(order):
            for it in units[j]['items']:
                if it[0] == 'op':
                    _, E, fn, reads, writes, inc = it
                    self._emit_op(E, fn, reads, writes, inc)
                else:
                    _, Q, out, in_, reads, writes, kw = it
                    self._emit_dma(Q, out, in_, reads, writes, **kw)

    def _emit_op(self, E, fn, reads=(), writes=(), inc=True):
        waits = self._deps(E, reads, writes)
        eng = self.e[E]
        for (sem, c) in waits[1:]:
            eng.wait_ge(sem, c)
        inst = fn()
        if waits:
            inst._wait_ge(waits[0][0], waits[0][1])
        if inc:
            self.cnt[E] += 1
            inst.then_inc(self.sem[E], 1)
            tag = (E, self.cnt[E])
        else:
            tag = (E, self.cnt[E] + 1)
        self._record(tag, reads, writes)
        return inst

    def _emit_dma(self, Q, out, in_, reads=(), writes=(), **kw):
        pool_ids = self.dpool[Q]
        i = pool_ids[self.drr[Q] % len(pool_ids)]
        self.drr[Q] += 1
        extra = []
        if self.dcnt[i] > 0:
            extra.append((('d', i), self.dcnt[i]))
        waits = self._deps(Q, reads, writes, extra)
        eng = self.e[Q]
        for (sem, c) in waits[1:]:
            eng.wait_ge(sem, c)
        inst = eng.dma_start(out=out, in_=in_, **kw)
        if waits:
            inst._wait_ge(waits[0][0], waits[0][1])
        self.dcnt[i] += 16
        inst.then_inc(self.dsem[i], 16)
        self._record((('d', i), self.dcnt[i]), reads, writes)
        return inst

    def barrier(self):
        for E in self.ENG:
            eng = self.e[E]
            for s in self.ENG:
                if s == E:
                    continue
                c = self.cnt[s]
                if c > self.seen[E].get(s, 0):
                    self.seen[E][s] = c
                    eng.wait_ge(self.sem[s], c)
            for i in range(NDMA):
                c = self.dcnt[i]
                if c > self.seen[E].get(('d', i), 0):
                    self.seen[E][('d', i)] = c
                    eng.wait_ge(self.dsem[i], c)

    def finish(self, E='sp'):
        eng = self.e[E]
        for i in range(NDMA):
            if self.dcnt[i] > 0:
                eng.wait_ge(self.dsem[i], self.dcnt[i])
        for s in self.ENG:
            if s != E and self.cnt[s] > 0:
                eng.wait_ge(self.sem[s], self.cnt[s])


def merge(gens):
    st = [[iter(g), 0.0, float(tot), True] for g, tot in gens]
    while any(s[3] for s in st):
        s = min((s for s in st if s[3]), key=lambda s: s[1] / s[2])
        try:
            w = next(s[0])
            s[1] += (w if w else 1.0)
        except StopIteration:
            s[3] = False


def build_nc():
    nc = bass.Bass("TRN2", target_bir_lowering=False)
    dram = lambda n, sh, kind="ExternalInput": nc.dram_tensor(n, sh, F32, kind=kind).ap()
    x = dram("x", [T, D])
    w_in = dram("w_in", [D, 3328])
    w_out = dram("w_out", [D, D])
    lbl = dram("lbl", [128, 8])
    hgw = dram("hgw", [128, 4])
    snk = dram("snk", [128, 8])
    lng = dram("lng", [128, D])
    lnb = dram("lnb", [128, D])
    cosd = dram("cosd", [128, NTT * 8])
    sind = dram("sind", [128, NTT * 8])
    maskA_d = dram("maskA", [128, 512])
    mask2_d = dram("mask2", [128, 512])
    y = dram("y", [T, D], kind="ExternalOutput")

    w_in_r = w_in.rearrange("(c p) n -> p c n", p=128)
    w_out_r = w_out.rearrange("(c p) n -> p c n", p=128)

    with ExitStack() as es:
        S = Sched(nc, es)
        S.begin()

        def sb(n, sh, dt=F32, stack=es):
            return stack.enter_context(nc.sbuf_tensor(n, sh, dt))

        XT = sb("XT", [128, 8, T], BF16)
        XB = [sb("XB%d" % i, [128, D], BF16) for i in range(6)]
        WG3 = sb("WG3", [128, 8, 1280], BF16)
        WH = [sb("WH%d" % i, [128, 8, 512], BF16) for i in range(2)]
        WOUT = sb("WOUT", [128, 8, D], BF16)
        WSTG = [sb("WSTG%d" % i, [128, D], F32) for i in range(1)]
        MIXT_H = sb("MIXT_H", [128, 4, T], BF16)
        IDF = sb("IDF", [128, 128], F32)
        IDENT = sb("IDENT", [128, 128], BF16)
        CST = sb("CST", [128, 512], F32)
        MASKA = sb("MASKA", [128, 512], BF16)
        MASK2 = sb("MASK2", [128, 512], BF16)
        RESETM = sb("RESETM", [128, 1024], BF16)
        COS = sb("COS", [128, NTT, 8], F32)
        SIN = sb("SIN", [128, NTT, 8], F32)
        LNG = sb("LNG", [128, D], F32)
        LNB = sb("LNB", [128, D], F32)
        LBL = sb("LBL", [128, 8], F32)
        HGW = sb("HGW", [128, 4], F32)
        SNK = sb("SNK", [128, 8], F32)
        ESINK2 = sb("ESINK2", [128, 8], F32)
        LBD = sb("LBD", [128, 4], F32)
        TL = sb("TL", [128, 4], F32)
        ACOL = sb("ACOL", [128, 4], F32)
        BCOL = sb("BCOL", [128, 4], F32)
        CCOL = sb("CCOL", [128, 4], F32)
        NHALF = sb("NHALF", [128, 8], F32)

        BK = [es.enter_context(nc.psum_tensor("BK%d" % i, [128, 512], F32)) for i in range(8)]
        BKb = [b[:].bitcast(BF16) for b in BK]
        bk = lambda i: 'BK%d' % i

        def load_wh(h):
            hb = h % 2
            for j, col0 in enumerate((0, 512, 1024, 1536)):
                S.dma('pool', WH[hb][:, :, j * 128:(j + 1) * 128],
                      w_in_r[:, :, col0 + h * 128: col0 + (h + 1) * 128],
                      writes=['WH%d_%d' % (hb, j)], nbytes=524288)

        def load_x_tile(tt):
            b = tt % 6
            S.dma('pool', XB[b][:], x[tt * 128:(tt + 1) * 128, :], writes=['XB%d' % b], nbytes=524288)

        S.dma('sp', LBL[:], lbl[:, :], writes=['LBL'])
        S.dma('sp', HGW[:], hgw[:, :], writes=['HGW'])
        S.dma('sp', SNK[:], snk[:, :], writes=['SNK'])
        S.dma('sp', COS[:].rearrange("p a b -> p (a b)"), cosd[:, :], writes=['COS'])
        S.dma('sp', SIN[:].rearrange("p a b -> p (a b)"), sind[:, :], writes=['SIN'])
        S.dma('sp', CST[:], maskA_d[:, :], writes=['CST'])
        S.op('dve', lambda: nc.vector.tensor_copy(out=MASKA[:], in_=CST[:]), reads=['CST'], writes=['MASKA'])
        S.dma('sp', CST[:], mask2_d[:, :], writes=['CST'])
        S.op('dve', lambda: nc.vector.tensor_copy(out=MASK2[:], in_=CST[:]), reads=['CST'], writes=['MASK2'])
        S.dma('sp', LNG[:], lng[:, :], writes=['LNG'])
        S.dma('sp', LNB[:], lnb[:, :], writes=['LNB'])

        load_wh(0)

        S.op('pool', lambda: nc.gpsimd.memset(IDF[:], 1.0), writes=['IDF'])
        S.op('pool', lambda: nc.gpsimd.affine_select(out=IDF[:], in_=IDF[:], pattern=[[-1, 128]],
                                                     compare_op=ALU.is_equal, fill=0.0, base=0,
                                                     channel_multiplier=1), reads=['IDF'], writes=['IDF'])
        S.op('dve', lambda: nc.vector.tensor_copy(out=IDENT[:], in_=IDF[:]), reads=['IDF'], writes=['IDENT'])
        S.op('pool', lambda: nc.gpsimd.memset(RESETM[:], 1.0), writes=['RESETM'])
        S.op('pool', lambda: nc.gpsimd.memset(RESETM[:].rearrange("p (a b) -> p a b", b=64)[:, :, 0:1], 0.0),
             writes=['RESETM'])
        S.op('pool', lambda: nc.gpsimd.memset(NHALF[:], -0.5), writes=['NHALF'])

        S.op('dve', lambda: nc.vector.tensor_tensor(out=LBD[:], in0=LBL[:, 0:4], in1=LBL[:, 4:8], op=ALU.subtract),
             reads=['LBL'], writes=['LBD'])
        S.op('act', lambda: nc.scalar.activation(out=TL[:], in_=LBD[:], func=AF.Tanh, scale=0.5),
             reads=['LBD'], writes=['TL'])
        S.op('dve', lambda: nc.vector.tensor_scalar(out=ACOL[:], in0=TL[:], scalar1=0.25, scalar2=0.75,
                                                    op0=ALU.mult, op1=ALU.add), reads=['TL'], writes=['ACOL'])
        S.op('dve', lambda: nc.vector.tensor_scalar(out=BCOL[:], in0=TL[:], scalar1=-0.25, scalar2=0.25,
                                                    op0=ALU.mult, op1=ALU.add), reads=['TL'], writes=['BCOL'])
        S.op('dve', lambda: nc.vector.tensor_scalar(out=CCOL[:], in0=TL[:], scalar1=0.125, scalar2=-0.125,
                                                    op0=ALU.mult, op1=ALU.add), reads=['TL'], writes=['CCOL'])
        S.op('act', lambda: nc.scalar.activation(out=ESINK2[:], in_=SNK[:], func=AF.Exp, bias=LN2),
             reads=['SNK'], writes=['ESINK2'])

        evac_rr = [0]

        def evac_copy(out, in_, reads, writes):
            evac_rr[0] ^= 1
            if evac_rr[0]:
                S.op('act', lambda: nc.scalar.copy(out=out, in_=in_), reads=reads, writes=writes, n=1024)
            else:
                S.op('dve', lambda: nc.vector.tensor_copy(out=out, in_=in_), reads=reads, writes=writes, n=1024)

        def phase0_tile(tt):
            load_x_tile(tt)
            b = tt % 6
            bank = tt % 2
            pv = BKb[bank].rearrange("p (a b) -> p a b", a=8)
            for c in range(8):
                S.op('pe', lambda c=c: nc.tensor.transpose(out=pv[:, c, :], in_=XB[b][:, c * 128:(c + 1) * 128],
                                                           identity=IDENT[:]),
                     reads=['XB%d' % b, 'IDENT'], writes=[bk(bank)], inc=(c == 7))
            evac_copy(XT[:, :, tt * 128:(tt + 1) * 128], pv[:, :, :], reads=[], writes=[bk(bank), 'XT%d' % tt])

        NH = 1024
        with ExitStack() as p1:
            TH = [sb("TH%d" % i, [128, NH], F32, p1) for i in range(2)]
            THQ = sb("THQ", [128, 512], F32, p1)
            THG = sb("THG", [128, 512], F32, p1)
            Q = [sb("Q%d" % i, [128, NH], F32, p1) for i in range(2)]
            LG = sb("LG", [128, NH], F32, p1)
            G = sb("G", [128, NH], F32, p1)
            QD = [sb("QD%d" % i, [128, NH], BF16, p1) for i in range(2)]
            KDT = [sb("KDT%d" % i, [128, NH], BF16, p1) for i in range(2)]
            KDTOK = sb("KDTOK", [128, 8, 128], BF16, p1)
            VH = [sb("VH%d" % i, [128, 8, 128], BF16, p1) for i in range(3)]
            GATEH = [sb("GATEH%d" % i, [128, 8, 128], BF16, p1) for i in range(3)]
            ORAW = [sb("ORAW%d" % i, [128, 8, 128], BF16, p1) for i in range(1)]
            SBF = sb("SBF", [128, 32, 128], BF16, p1)
            R = [sb("R%d" % i, [128, 128], F32, p1) for i in range(4)]
            DEC = [sb("DEC%d" % i, [128, 32], F32, p1) for i in range(2)]
            AM = [sb("AM%d" % i, [128, 512], BF16, p1) for i in range(2)]
            SQ = [sb("SQ%d" % i, [128, 512], F32, p1) for i in range(2)]
            SS = sb("SS", [128, 8], F32, p1)
            MS = sb("MS", [128, 8], F32, p1)
            RSTD = sb("RSTD", [128, 8], F32, p1)
            MIXTOK = sb("MIXTOK", [128, 8, 128], BF16, p1)

            PROJ_BANKS = (0, 1, 7)
            proj_rr = [0]

            def next_proj_bank():
                proj_rr[0] += 1
                return PROJ_BANKS[proj_rr[0] % 3]

            def stageA(u):
                h, half = divmod(u, 2)
                hb = h % 2
                u2, u3 = u % 2, u % 3
                t0 = half * NH
                xt_keys = ['XT%d' % (half * 8 + i) for i in range(8)]
                for tp in range(4):
                    bank = next_proj_bank()
                    for j2 in range(2):
                        tt = tp * 2 + j2
                        for c in range(8):
                            S.op('pe', lambda c=c, j2=j2, tt=tt, bank=bank: nc.tensor.matmul(
                                BK[bank][:, j2 * 256:(j2 + 1) * 256],
                                lhsT=XT[:, c, t0 + tt * 128: t0 + (tt + 1) * 128],
                                rhs=WH[hb][:, c, 256:512],
                                start=(c == 0), stop=(c == 7)),
                                reads=['WH%d_2' % hb, 'WH%d_3' % hb, xt_keys[tt]], writes=[bk(bank)],
                                inc=(c == 7 and j2 == 1))
                    pv = BK[bank][:, :].rearrange("p (a b) -> p a b", a=2)
                    S.op('act', lambda tp=tp, pv=pv: nc.scalar.copy(
                        out=VH[u3][:, tp * 2:(tp + 1) * 2, :], in_=pv[:, :, 0:128]),
                        writes=[bk(bank), 'VH%d' % u3])
                    S.op('act', lambda pv=pv: nc.scalar.activation(
                        out=THG[:, 0:256].rearrange("p (a b) -> p a b", a=2), in_=pv[:, :, 128:256],
                        func=AF.Tanh, scale=0.5),
                        writes=[bk(bank), 'THG'])
                    S.op('dve', lambda tp=tp, pv=pv: nc.vector.scalar_tensor_tensor(
                        out=GATEH[u3][:, tp * 2:(tp + 1) * 2, :],
                        in0=THG[:, 0:256].rearrange("p (a b) -> p a b", a=2), scalar=1.0, in1=pv[:, :, 128:256],
                        op0=ALU.add, op1=ALU.mult),
                        reads=['THG'], writes=[bk(bank), 'GATEH%d' % u3])
                    yield 1.7
                for j in (1, 0):
                    for tg in range(2):
                        bank = next_proj_bank()
                        for c in range(8):
                            S.op('pe', lambda c=c, j=j, tg=tg, bank=bank: nc.tensor.matmul(
                                BK[bank][:, :], lhsT=WH[hb][:, c, j * 128:(j + 1) * 128],
                                rhs=XT[:, c, t0 + tg * 512: t0 + (tg + 1) * 512],
                                start=(c == 0), stop=(c == 7)),
                                reads=['WH%d_%d' % (hb, j)] + xt_keys[tg * 4:(tg + 1) * 4],
                                writes=[bk(bank)], inc=(c == 7), n=512)
                        if j == 1:
                            S.op('act', lambda tg=tg, bank=bank: nc.scalar.activation(
                                out=TH[u2][:, tg * 512:(tg + 1) * 512], in_=BK[bank][:, :], func=AF.Tanh, scale=0.5),
                                writes=[bk(bank), 'TH%d' % u2])
                        else:
                            S.op('act', lambda bank=bank: nc.scalar.activation(
                                out=THQ[:], in_=BK[bank][:, :], func=AF.Tanh, scale=0.5),
                                writes=[bk(bank), 'THQ'])
                            S.op('dve', lambda tg=tg, bank=bank: nc.vector.scalar_tensor_tensor(
                                out=Q[u2][:, tg * 512:(tg + 1) * 512], in0=THQ[:], scalar=1.0, in1=BK[bank][:, :],
                                op0=ALU.add, op1=ALU.mult),
                                reads=['THQ'], writes=[bk(bank), 'Q%d' % u2])
                        yield 1.7

            def stageB(u):
                h, half = divmod(u, 2)
                u2 = u % 2
                hp = h % 2
                S.op('act', lambda: nc.scalar.activation(out=LG[:], in_=TH[u2][:], func=AF.Ln,
                                                         scale=BCOL[:, h:h + 1], bias=ACOL[:, h:h + 1]),
                     reads=['TH%d' % u2, 'BCOL', 'ACOL'], writes=['LG'], n=1024)
                yield 1.3
                S.op('dve', lambda: nc.vector.tensor_tensor_scan(out=G[:], data0=RESETM[:], data1=LG[:], initial=0.0,
                                                                 op0=ALU.mult, op1=ALU.add),
                     reads=['RESETM', 'LG'], writes=['G'], n=2048)
                yield 2.3
                S.op('act', lambda: nc.scalar.activation(out=LG[:], in_=G[:], func=AF.Exp),
                     reads=['G'], writes=['LG'], n=1024)
                yield 1.1
                S.op('act', lambda: nc.scalar.activation(out=G[:], in_=G[:], func=AF.Exp, scale=-1.0),
                     reads=['G'], writes=['G'], n=1024)
                S.op('dve', lambda: nc.vector.tensor_copy(
                    out=DEC[hp][:, half * 16:(half + 1) * 16],
                    in_=LG[:].rearrange("p (a b) -> p a b", b=64)[:, :, 63]),
                    reads=['LG'], writes=['DEC%d' % hp], n=16)
                yield 1.1
                S.op('dve', lambda: nc.vector.scalar_tensor_tensor(out=QD[u2][:], in0=Q[u2][:], scalar=CCOL[:, h:h + 1],
                                                                   in1=LG[:], op0=ALU.mult, op1=ALU.mult),
                     reads=['Q%d' % u2, 'CCOL', 'LG'], writes=['QD%d' % u2], n=1024)
                yield 1.2
                S.op('dve', lambda: nc.vector.scalar_tensor_tensor(out=KDT[u2][:], in0=TH[u2][:], scalar=1.0,
                                                                   in1=G[:], op0=ALU.subtract, op1=ALU.mult),
                     reads=['TH%d' % u2, 'G'], writes=['KDT%d' % u2], n=1024)
                yield 1.2

            def stageC(u):
                h, half = divmod(u, 2)
                u2, u3 = u % 2, u % 3
                hp = h % 2
                kdt, qd, vh, gateh, dec = KDT[u2], QD[u2], VH[u3], GATEH[u3], DEC[hp]
                kK, kQ, kV, kG, kD = 'KDT%d' % u2, 'QD%d' % u2, 'VH%d' % u3, 'GATEH%d' % u3, 'DEC%d' % hp
                pv = BKb[2].rearrange("p (a b) -> p a b", a=8)
                for tq in range(2):
                    for tt in range(tq * 4, tq * 4 + 4):
                        S.op('pe', lambda tt=tt: nc.tensor.transpose(out=pv[:, tt, :], in_=kdt[:, tt * 128:(tt + 1) * 128],
                                                                     identity=IDENT[:]),
                             reads=[kK, 'IDENT'], writes=[bk(2)], inc=(tt % 4 == 3))
                    S.op('act', lambda tq=tq: nc.scalar.copy(out=KDTOK[:, tq * 4:(tq + 1) * 4, :],
                                                             in_=pv[:, tq * 4:(tq + 1) * 4, :]),
                         writes=[bk(2), 'KDTOK%d' % tq], n=512)
                if half == 0:
                    S.op('pool', lambda: nc.gpsimd.memset(SBF[:, 0, :], 0.0), writes=['SBF0'])
                yield 2.0
                for tq in range(2):
                    for jj in range(4):
                        tt = tq * 4 + jj
                        S.op('pe', lambda jj=jj, tt=tt: nc.tensor.matmul(
                            BK[3][:, jj * 128:(jj + 1) * 128], lhsT=kdt[:, tt * 128:(tt + 1) * 128],
                            rhs=qd[:, tt * 128:(tt + 1) * 128], start=True, stop=True),
                            reads=[kK, kQ], writes=[bk(3)], inc=(jj == 3))
                    ab = tq % 2
                    S.op('dve', lambda ab=ab: nc.vector.tensor_tensor(out=AM[ab][:], in0=BK[3][:, :], in1=MASKA[:],
                                                                      op=ALU.mult),
                         reads=['MASKA'], writes=[bk(3), 'AM%d' % ab])
                    for jj in range(4):
                        tt = tq * 4 + jj
                        for cj in range(2):
                            S.op('pe', lambda tt=tt, cj=cj, jj=jj: nc.tensor.matmul(
                                BK[4 + cj][:, jj * 128:(jj + 1) * 128],
                                lhsT=KDTOK[cj * 64:(cj + 1) * 64, tt, :],
                                rhs=vh[cj * 64:(cj + 1) * 64, tt, :], start=True, stop=True),
                                reads=['KDTOK%d' % tq, kV], writes=[bk(4 + cj)],
                                inc=(jj == 3))
                    yield 1.5
                    for jj in range(4):
                        for cj in range(2):
                            n = half * 16 + tq * 8 + jj * 2 + cj
                            ubank = 4 + cj
                            col = jj * 128
                            rb = n % 4
                            rp = (n - 1) % 4
                            if n == 0:
                                S.op('dve', lambda col=col, ubank=ubank: nc.vector.tensor_copy(
                                    out=R[0][:], in_=BK[ubank][:, col:col + 128]),
                                    writes=[bk(ubank), 'R0'], n=128, cost=0.36)
                            else:
                                S.op('dve', lambda col=col, ubank=ubank, rb=rb, rp=rp, n=n: nc.vector.scalar_tensor_tensor(
                                    out=R[rb][:], in0=R[rp][:], scalar=dec[:, n - 1:n],
                                    in1=BK[ubank][:, col:col + 128], op0=ALU.mult, op1=ALU.add),
                                    reads=['R%d' % rp, kD], writes=[bk(ubank), 'R%d' % rb], n=128, cost=0.36)
                            if n < 31:
                                S.op('pool', lambda rb=rb, n=n: nc.gpsimd.tensor_scalar(
                                    out=SBF[:, n + 1, :], in0=R[rb][:], scalar1=dec[:, n:n + 1], scalar2=1.0,
                                    op0=ALU.mult, op1=ALU.mult),
                                    reads=['R%d' % rb, kD], writes=['SBF%d' % (n + 1)], n=128, cost=0.4)
                        yield 0.9
                    obank = 6
                    for jj in range(4):
                        tt = tq * 4 + jj
                        S.op('pe', lambda jj=jj, tt=tt, ab=ab, obank=obank: nc.tensor.matmul(
                            BK[obank][:, jj * 128:(jj + 1) * 128], lhsT=AM[ab][:, jj * 128:(jj + 1) * 128],
                            rhs=vh[:, tt, :], start=True, stop=False, skip_group_check=True),
                            reads=['AM%d' % ab, kV], writes=[bk(obank)], inc=False)
                        for cj in range(2):
                            n = half * 16 + tt * 2 + cj
                            S.op('pe', lambda jj=jj, tt=tt, cj=cj, n=n, obank=obank: nc.tensor.matmul(
                                BK[obank][cj * 64:(cj + 1) * 64, jj * 128:(jj + 1) * 128],
                                lhsT=qd[:, tt * 128 + cj * 64: tt * 128 + (cj + 1) * 64],
                                rhs=SBF[:, n, :], start=False, stop=(cj == 1), skip_group_check=True),
                                reads=[kQ, 'SBF%d' % n], writes=[bk(obank)],
                                inc=(jj == 3 and cj == 1))
                    pvo = BK[obank][:, :].rearrange("p (a b) -> p a b", a=4)
                    S.op('act', lambda obank=obank, tq=tq: nc.scalar.activation(out=SQ[tq][:], in_=BK[obank][:, :],
                                                                                func=AF.Square),
                         writes=[bk(obank), 'SQ%d' % tq])
                    S.op('act', lambda tq=tq, pvo=pvo: nc.scalar.copy(out=ORAW[0][:, tq * 4:(tq + 1) * 4, :], in_=pvo),
                         writes=[bk(obank), 'ORAW%d' % tq])
                    S.op('dve', lambda tq=tq: nc.vector.tensor_reduce(
                        out=SS[:, tq * 4:(tq + 1) * 4], in_=SQ[tq][:].rearrange("p (a b) -> p a b", a=4),
                        axis=AX.X, op=ALU.add), reads=['SQ%d' % tq], writes=['SS%d' % tq])
                    yield 2.0
                    S.op('dve', lambda tq=tq: nc.vector.tensor_scalar(
                        out=MS[:, tq * 4:(tq + 1) * 4], in0=SS[:, tq * 4:(tq + 1) * 4], scalar1=1.0 / 128.0,
                        scalar2=RMS_EPS, op0=ALU.mult, op1=ALU.add), reads=['SS%d' % tq], writes=['MS%d' % tq], n=4)
                    S.op('pool', lambda tq=tq: nc.gpsimd.tensor_tensor(
                        out=RSTD[:, tq * 4:(tq + 1) * 4], in0=MS[:, tq * 4:(tq + 1) * 4], in1=NHALF[:, 0:4],
                        op=ALU.pow), reads=['MS%d' % tq, 'NHALF'], writes=['RSTD%d' % tq], cost=1.0)
                    for tt in range(tq * 4, tq * 4 + 4):
                        S.op('dve', lambda tt=tt: nc.vector.scalar_tensor_tensor(
                            out=MIXTOK[:, tt, :], in0=ORAW[0][:, tt, :], scalar=RSTD[:, tt:tt + 1],
                            in1=gateh[:, tt, :], op0=ALU.mult, op1=ALU.mult),
                            reads=['ORAW%d' % tq, 'RSTD%d' % tq, kG], writes=['MIXTOK%d' % tq], n=128, cost=0.36)
                    for tt in range(tq * 4, tq * 4 + 4):
                        S.op('pe', lambda tt=tt: nc.tensor.transpose(out=pv[:, tt, :], in_=MIXTOK[:, tt, :],
                                                                     identity=IDENT[:]),
                             reads=['MIXTOK%d' % tq, 'IDENT'], writes=[bk(2)], inc=(tt % 4 == 3))
                    S.op('act', lambda tq=tq: nc.scalar.copy(
                        out=MIXT_H[:, h, half * NH + tq * 512: half * NH + (tq + 1) * 512].rearrange(
                            "p (a b) -> p a b", a=4),
                        in_=pv[:, tq * 4:(tq + 1) * 4, :]),
                        writes=[bk(2)] + ['MIXTH%d' % (half * 8 + tq * 4 + i) for i in range(4)], n=512)
                    yield 2.0

            def prefetch_rest():
                S.dma('pool', WG3[:, :, 0:512], w_in_r[:, :, 2048:2560], writes=['WG3q'], nbytes=2097152)
                S.dma('pool', WG3[:, :, 512:768], w_in_r[:, :, 2560:2816], writes=['WG3kv'], nbytes=1048576)
                S.dma('pool', WG3[:, :, 768:1280], w_in_r[:, :, 2816:3328], writes=['WG3g'], nbytes=2097152)
                S.dma('pool', WOUT[:, 4:8, :], w_out_r[:, 4:8, :], writes=['WOUT_A'], nbytes=2097152)
                for c in range(4):
                    b = 0
                    S.dma('sp', WSTG[b][:], w_out[c * 128:(c + 1) * 128, :], writes=['WSTG%d' % b], nbytes=524288)
                    S.op('pool', lambda c=c, b=b: nc.gpsimd.tensor_scalar(
                        out=WOUT[:, c, :], in0=WSTG[b][:], scalar1=HGW[:, c:c + 1], scalar2=0.5,
                        op0=ALU.mult, op1=ALU.mult), reads=['WSTG%d' % b, 'HGW'], writes=['WOUT_H%d' % c], n=1024)

            for tt in range(8):
                phase0_tile(tt)
            load_wh(1)
            for tt in range(8, 16):
                phase0_tile(tt)
            for u in range(8):
                h, half = divmod(u, 2)
                if half == 0 and h >= 2:
                    load_wh(h)
                for _ in stageA(u):
                    pass
                for _ in stageB(u):
                    pass
                for _ in stageC(u):
                    pass
                if u == 3:
                    prefetch_rest()
            S.end()
            S.barrier()

        with ExitStack() as p2:
            RAW = [sb("RAW%d" % i, [128, 12, 64], F32, p2) for i in range(2)]
            QKTOK = [sb("QKTOK%d" % i, [128, 12, 64], BF16, p2) for i in range(2)]
            ROPT = [sb("ROPT%d" % i, [128, 12, 8], F32, p2) for i in range(4)]
            QT = [sb("QT%d" % i, [128, 4, 128], BF16, p2) for i in range(2)]
            KT2 = sb("KT2", [128, 3, 2, 128], BF16, p2)
            VAUG = sb("VAUG", [128, 3, 2, 66], BF16, p2)
            THA = [sb("THA%d" % i, [128, 512], F32, p2) for i in range(2)]
            AG = [sb("AG%d" % i, [128, 512], F32, p2) for i in range(2)]
            OZ = [sb("OZ%d" % i, [128, D], F32, p2) for i in range(2)]
            GATEA = [sb("GATEA%d" % i, [128, 512], BF16, p2) for i in range(2)]
            PT = [sb("PT%d" % i, [128, 512], BF16, p2) for i in range(4)]
            DEN = [sb("DEN%d" % i, [128, 4], F32, p2) for i in range(2)]
            REC = [sb("REC%d" % i, [128, 4], F32, p2) for i in range(2)]
            TMPN = [sb("TMPN%d" % i, [128, 4, 64], F32, p2) for i in range(2)]
            MIXA = [sb("MIXA%d" % i, [128, 512], BF16, p2) for i in range(2)]
            MIXTA = [sb("MIXTA%d" % i, [128, 4, 128], BF16, p2) for i in range(2)]
            XRES = [sb("XRES%d" % i, [128, D], F32, p2) for i in range(2)]
            ZN = [sb("ZN%d" % i, [128, D], F32, p2) for i in range(2)]
            STATS = sb("STATS", [128, 2, 6], F32, p2)
            MV = [sb("MV%d" % i, [128, 2], F32, p2) for i in range(2)]
            VE = sb("VE", [128, 1], F32, p2)
            RS = [sb("RS%d" % i, [128, 1], F32, p2) for i in range(2)]

            S.begin()
            S.op('pool', lambda: nc.gpsimd.memset(VAUG[:, :, :, 64:65], 1.0), writes=['VAUG0', 'VAUG1', 'VAUG2'])
            pt_rr = [0]
            SB2 = (5, 6)
            PVB = (3, 4)
            OPB = (1, 2)
            TB = 7

            def stageA1(n):
                b = n % 2
                slot = n % 3
                S.dma('sp', XRES[b][:], x[n * 128:(n + 1) * 128, :], writes=['XRES%d' % b], nbytes=524288)
                def proj(c0, c1, key):
                    for c in range(8):
                        S.op('pe', lambda c=c, c0=c0, c1=c1: nc.tensor.matmul(
                            BK[0][:, 0:c1 - c0], lhsT=XT[:, c, n * 128:(n + 1) * 128], rhs=WG3[:, c, c0:c1],
                            start=(c == 0), stop=(c == 7)),
                            reads=[key, 'XT%d' % n], writes=[bk(0)], inc=(c == 7), n=c1 - c0)
                proj(0, 512, 'WG3q')
                S.op('act', lambda: nc.scalar.copy(out=RAW[b][:, 0:8, :],
                                                   in_=BK[0][:, :].rearrange("p (a b) -> p a b", a=8)),
                     writes=[bk(0), 'RAW%d' % b])
                proj(512, 768, 'WG3kv')
                S.op('act', lambda: nc.scalar.copy(
                    out=RAW[b][:, 8:12, :].rearrange("p (g r) d -> p g r d", g=2),
                    in_=BK[0][:, 0:128].rearrange("p (g d) -> p g d", g=2).unsqueeze(2).to_broadcast([128, 2, 2, 64])),
                    writes=[bk(0), 'RAW%d' % b])
                S.op('act', lambda: nc.scalar.copy(out=VAUG[:, slot, :, 0:64],
                                                   in_=BK[0][:, 128:256].rearrange("p (g d) -> p g d", g=2)),
                     writes=[bk(0), 'VAUG%d' % slot], n=128)
                proj(768, 1280, 'WG3g')
                S.op('act', lambda: nc.scalar.activation(out=THA[b][:], in_=BK[0][:, :], func=AF.Tanh, scale=0.5),
                     writes=[bk(0), 'THA%d' % b])
                S.op('act', lambda: nc.scalar.copy(out=AG[b][:], in_=BK[0][:, :]), writes=[bk(0), 'AG%d' % b])
                S.op('dve', lambda: nc.vector.scalar_tensor_tensor(out=GATEA[b][:], in0=THA[b][:], scalar=1.0,
                                                                   in1=AG[b][:], op0=ALU.add, op1=ALU.mult),
                     reads=['THA%d' % b, 'AG%d' % b], writes=['GATEA%d' % b])
                S.op('act', lambda: nc.scalar.copy(out=QKTOK[b][:, :, 16:64], in_=RAW[b][:, :, 16:64]),
                     reads=['RAW%d' % b], writes=['QKTOK%d' % b])
                cosb = COS[:, n, :].unsqueeze(1).to_broadcast([128, 12, 8])
                sinb = SIN[:, n, :].unsqueeze(1).to_broadcast([128, 12, 8])
                x1 = RAW[b][:, :, 0:8]
                x2 = RAW[b][:, :, 8:16]
                S.op('pool', lambda: nc.gpsimd.tensor_tensor(out=ROPT[0][:], in0=x1, in1=cosb, op=ALU.mult),
                     reads=['RAW%d' % b, 'COS'], writes=['ROPT0'], n=96)
                S.op('pool', lambda: nc.gpsimd.tensor_tensor(out=ROPT[1][:], in0=x2, in1=sinb, op=ALU.mult),
                     reads=['RAW%d' % b, 'SIN'], writes=['ROPT1'], n=96)
                S.op('pool', lambda: nc.gpsimd.tensor_tensor(out=QKTOK[b][:, :, 0:8], in0=ROPT[0][:], in1=ROPT[1][:],
                                                            op=ALU.subtract),
                     reads=['ROPT0', 'ROPT1'], writes=['QKTOK%d' % b], n=96)
                S.op('pool', lambda: nc.gpsimd.tensor_tensor(out=ROPT[2][:], in0=x2, in1=cosb, op=ALU.mult),
                     reads=['RAW%d' % b, 'COS'], writes=['ROPT2'], n=96)
                S.op('pool', lambda: nc.gpsimd.tensor_tensor(out=ROPT[3][:], in0=x1, in1=sinb, op=ALU.mult),
                     reads=['RAW%d' % b, 'SIN'], writes=['ROPT3'], n=96)
                S.op('pool', lambda: nc.gpsimd.tensor_tensor(out=QKTOK[b][:, :, 8:16], in0=ROPT[2][:], in1=ROPT[3][:],
                                                            op=ALU.add),
                     reads=['ROPT2', 'ROPT3'], writes=['QKTOK%d' % b], n=96)

            def stageA2(n):
                b = n % 2
                slot = n % 3
                pvq = BKb[TB].rearrange("p (a b) -> p a b", a=8)
                qk2 = QKTOK[b][:].rearrange("p (a r) d -> p a (r d)", r=2)
                for i in range(6):
                    S.op('pe', lambda i=i: nc.tensor.transpose(out=pvq[:, i, :], in_=qk2[:, i, :], identity=IDENT[:]),
                         reads=['QKTOK%d' % b, 'IDENT'], writes=[bk(TB)], inc=(i == 5))
                S.op('act', lambda: nc.scalar.copy(out=QT[b][:, :, :], in_=pvq[:, 0:4, :]),
                     writes=[bk(TB), 'QT%d' % b])
                S.op('act', lambda: nc.scalar.copy(out=KT2[:, slot, :, :], in_=pvq[:, 4:6, :]),
                     writes=[bk(TB), 'KT2_%d' % slot], n=256)

            def stageB1(n):
                b = n % 2
                slot = n % 3
                pslot = (n - 1) % 3
                kts = ([(0, pslot)] if n > 0 else []) + [(1, slot)]
                pis = []
                for g in range(2):
                    pi2 = []
                    for uu in range(2):
                        pi2.append(pt_rr[0] % 4)
                        pt_rr[0] += 1
                    pis.append(pi2)
                    for idx, (kt, ks) in enumerate(kts):
                        for uu in range(2):
                            S.op('pe', lambda uu=uu, ks=ks, kt=kt, g=g: nc.tensor.matmul(
                                BK[SB2[uu]][:, kt * 256:(kt + 1) * 256],
                                lhsT=KT2[uu * 64:(uu + 1) * 64, ks, g, :],
                                rhs=QT[b][uu * 64:(uu + 1) * 64, 2 * g:2 * g + 2, :], start=True, stop=True),
                                reads=['KT2_%d' % ks, 'QT%d' % b], writes=[bk(SB2[uu])],
                                inc=(idx == len(kts) - 1), n=256)
                    for uu in range(2):
                        if n > 0:
                            sel = lambda ap: ap
                        else:
                            sel = lambda ap: ap[:, 256:512]
                        S.op('act', lambda uu=uu, pi2=pi2, sel=sel: nc.scalar.activation(
                            out=sel(PT[pi2[uu]][:]), in_=sel(BK[SB2[uu]][:, :]), func=AF.Exp, scale=0.125),
                            writes=[bk(SB2[uu]), 'PT%d' % pi2[uu]])
                        S.op('dve', lambda uu=uu, pi2=pi2, sel=sel: nc.vector.tensor_tensor(
                            out=sel(PT[pi2[uu]][:]), in0=sel(PT[pi2[uu]][:]), in1=sel(MASK2[:]), op=ALU.mult),
                            reads=['MASK2'], writes=['PT%d' % pi2[uu]], cost=0.45)
                for g in range(2):
                    pi2 = pis[g]
                    for pl in range(2):
                        p = g * 2 + pl
                        for uu in range(2):
                            hh = 2 * p + uu
                            vbank = PVB[hh // 4]
                            col = (hh % 4) * 66
                            for idx, (kt, ks) in enumerate(kts):
                                sl = kt * 2 + pl
                                S.op('pe', lambda sl=sl, ks=ks, uu=uu, vbank=vbank, col=col, idx=idx, g=g, pi2=pi2:
                                     nc.tensor.matmul(
                                         BK[vbank][:, col:col + 65], lhsT=PT[pi2[uu]][:, sl * 128:(sl + 1) * 128],
                                         rhs=VAUG[:, ks, g, 0:65], start=(idx == 0), stop=(idx == len(kts) - 1)),
                                     reads=['PT%d' % pi2[uu], 'VAUG%d' % ks], writes=[bk(vbank)],
                                     inc=(idx == len(kts) - 1))
                for vb in range(2):
                    vbank = PVB[vb]
                    pvv = BK[vbank][:, 0:264].rearrange("p (a b) -> p a b", a=4)
                    S.op('dve', lambda vb=vb, pvv=pvv: nc.vector.scalar_tensor_tensor(
                        out=DEN[vb][:], in0=pvv[:, :, 64], scalar=2.0,
                        in1=ESINK2[:, vb * 4:(vb + 1) * 4], op0=ALU.mult, op1=ALU.add),
                        reads=['ESINK2'], writes=[bk(vbank), 'DEN%d' % vb], n=4)
                    S.op('dve', lambda vb=vb: nc.vector.reciprocal(out=REC[vb][:], in_=DEN[vb][:]),
                         reads=['DEN%d' % vb], writes=['REC%d' % vb], n=32)
                    S.op('dve', lambda vb=vb, pvv=pvv: nc.vector.tensor_tensor(
                        out=TMPN[vb][:], in0=pvv[:, :, 0:64],
                        in1=REC[vb][:].unsqueeze(2).to_broadcast([128, 4, 64]), op=ALU.mult),
                        reads=['REC%d' % vb], writes=[bk(vbank), 'TMPN%d' % vb], n=256)
                    S.op('dve', lambda vb=vb: nc.vector.tensor_tensor(
                        out=MIXA[b][:, vb * 256:(vb + 1) * 256].rearrange("p (a b) -> p a b", a=4),
                        in0=TMPN[vb][:],
                        in1=GATEA[b][:, vb * 256:(vb + 1) * 256].rearrange("p (a b) -> p a b", a=4), op=ALU.mult),
                        reads=['TMPN%d' % vb, 'GATEA%d' % b], writes=['MIXA%d' % b], n=256)

            def stageB2(n):
                b = n % 2
                pvm = BKb[TB].rearrange("p (a b) -> p a b", a=8)
                for c in range(4):
                    S.op('pe', lambda c=c: nc.tensor.transpose(out=pvm[:, c, :], in_=MIXA[b][:, c * 128:(c + 1) * 128],
                                                               identity=IDENT[:]),
                         reads=['MIXA%d' % b, 'IDENT'], writes=[bk(TB)], inc=(c == 3))
                S.op('act', lambda: nc.scalar.copy(out=MIXTA[b][:, :, :], in_=pvm[:, 0:4, :]),
                     writes=[bk(TB), 'MIXTA%d' % b])
                for hf in range(2):
                    zb = OPB[hf]
                    for c in range(8):
                        if c < 4:
                            lhsT = MIXT_H[:, c, n * 128:(n + 1) * 128]
                            rk = ['MIXTH%d' % n, 'WOUT_H%d' % c]
                        else:
                            lhsT = MIXTA[b][:, c - 4, :]
                            rk = ['MIXTA%d' % b, 'WOUT_A']
                        S.op('pe', lambda c=c, lhsT=lhsT, zb=zb, hf=hf: nc.tensor.matmul(
                            BK[zb][:, :], lhsT=lhsT, rhs=WOUT[:, c, hf * 512:(hf + 1) * 512],
                            start=(c == 0), stop=(c == 7)),
                            reads=rk, writes=[bk(zb)], inc=(c == 7), n=512)
                    S.op('act', lambda hf=hf, zb=zb: nc.scalar.copy(out=OZ[b][:, hf * 512:(hf + 1) * 512],
                                                                    in_=BK[zb][:, :]),
                         writes=[bk(zb), 'OZ%d' % b])

            def stageB3a(n):
                b = n % 2
                for hf in range(2):
                    zb = PVB[hf]
                    S.op('dve', lambda hf=hf, zb=zb: nc.vector.scalar_tensor_tensor(
                        out=XRES[b][:, hf * 512:(hf + 1) * 512], in0=XRES[b][:, hf * 512:(hf + 1) * 512],
                        scalar=DN_ALPHA, in1=OZ[b][:, hf * 512:(hf + 1) * 512], op0=ALU.mult, op1=ALU.add),
                        reads=['OZ%d' % b], writes=['XRES%d' % b])
                for hf in range(2):
                    S.op('dve', lambda hf=hf: nc.vector.bn_stats(out=STATS[:, hf, :],
                                                                 in_=XRES[b][:, hf * 512:(hf + 1) * 512]),
                         reads=['XRES%d' % b], writes=['STATS'])
                S.op('dve', lambda: nc.vector.bn_aggr(out=MV[b][:], in_=STATS[:].rearrange("p a b -> p (a b)")),
                     reads=['STATS'], writes=['MV%d' % b], n=12)
                S.op('dve', lambda: nc.vector.tensor_scalar(out=VE[:], in0=MV[b][:, 1:2], scalar1=LN_EPS, scalar2=None,
                                                            op0=ALU.add), reads=['MV%d' % b], writes=['VE'], n=1)
                S.op('pool', lambda: nc.gpsimd.tensor_tensor(out=RS[b][:], in0=VE[:], in1=NHALF[:, 0:1], op=ALU.pow),
                     reads=['VE', 'NHALF'], writes=['RS%d' % b], cost=0.55)

            def stageB3b(n):
                b = n % 2
                S.op('dve', lambda: nc.vector.scalar_tensor_tensor(
                    out=XRES[b][:], in0=XRES[b][:], scalar=MV[b][:, 0:1], in1=LNG[:],
                    op0=ALU.subtract, op1=ALU.mult),
                    reads=['MV%d' % b, 'LNG'], writes=['XRES%d' % b], n=1024)
                S.op('dve', lambda: nc.vector.scalar_tensor_tensor(
                    out=ZN[b][:], in0=XRES[b][:], scalar=RS[b][:, 0:1], in1=LNB[:],
                    op0=ALU.mult, op1=ALU.add),
                    reads=['XRES%d' % b, 'RS%d' % b, 'LNB'], writes=['ZN%d' % b], n=1024)
                S.dma('sp', y[n * 128:(n + 1) * 128, :], ZN[b][:], reads=['ZN%d' % b], nbytes=524288)

            for i in range(NTT):
                stageA1(i)
                stageA2(i)
                stageB1(i)
                stageB2(i)
                stageB3a(i)
                stageB3b(i)
            S.end()
            S.finish('sp')
    return nc


_NC_CACHE = {}


def _consts():
    ROPE_THETA = 500000.0
    ROPE_DIM = 16
    pos = np.arange(T, dtype=np.float32)
    inv_freq = (ROPE_THETA ** (-np.arange(0, ROPE_DIM, 2, dtype=np.float32) / ROPE_DIM)).astype(np.float32)
    ang = (pos[:, None] * inv_freq[None, :]).astype(np.float32)
    cos = np.cos(ang).astype(np.float32)
    sin = np.sin(ang).astype(np.float32)
    lay = lambda a: np.ascontiguousarray(a.reshape(NTT, 128, 8).transpose(1, 0, 2).reshape(128, NTT * 8))
    s = np.arange(128)[:, None]
    t = np.arange(128)[None, :]
    mA = ((s // 64 == t // 64) & (s <= t)).astype(np.float32)
    maskA = np.tile(mA, (1, 4))
    mprev = (s > t).astype(np.float32)
    mcur = (s <= t).astype(np.float32)
    mask2 = np.concatenate([mprev, mprev, mcur, mcur], axis=1)
    return lay(cos), lay(sin), np.ascontiguousarray(maskA), np.ascontiguousarray(mask2)


def kernel(x, w_in, lb_logits, hg_norm_w, sinks, w_out, ln_g, ln_b):
    x = np.asarray(x, dtype=np.float32)
    B = x.shape[0]
    if 'nc' not in _NC_CACHE:
        _NC_CACHE['nc'] = build_nc()
    nc = _NC_CACHE['nc']
    cosd, sind, maskA, mask2 = _consts()
    w_in0 = np.ascontiguousarray(np.asarray(w_in, np.float32)[0])
    w_out0 = np.ascontiguousarray(np.asarray(w_out, np.float32)[0])
    lbl = np.ascontiguousarray(np.asarray(lb_logits, np.float32).reshape(2, 4, 128).transpose(2, 0, 1).reshape(128, 8))
    hgw = np.ascontiguousarray(np.asarray(hg_norm_w, np.float32).reshape(4, 128).T)
    snk = np.ascontiguousarray(np.broadcast_to(np.asarray(sinks, np.float32).reshape(1, 8), (128, 8)))
    lng = np.ascontiguousarray(np.broadcast_to(np.asarray(ln_g, np.float32).reshape(1, D), (128, D)))
    lnb = np.ascontiguousarray(np.broadcast_to(np.asarray(ln_b, np.float32).reshape(1, D), (128, D)))
    common = dict(w_in=w_in0, w_out=w_out0, lbl=lbl, hgw=hgw, snk=snk, lng=lng, lnb=lnb,
                  cosd=cosd, sind=sind, maskA=maskA, mask2=mask2)
    in_maps = [dict(common, x=np.ascontiguousarray(x[b])) for b in range(B)]
    res = run_bass_kernel_spmd(nc, in_maps, core_ids=list(range(B)))
    return np.stack([np.asarray(r["y"], dtype=np.float32) for r in res.results], axis=0)
```

```python
import numpy as np
from contextlib import ExitStack
import concourse.bass as bass
import concourse.mybir as mybir
from concourse.bass_utils import run_bass_kernel_spmd

F32 = mybir.dt.float32
BF16 = mybir.dt.bfloat16
AF = mybir.ActivationFunctionType
ALU = mybir.AluOpType
AX = mybir.AxisListType

T = 2048
D = 1024
NTT = 16
DN_ALPHA = 2.0 ** 0.25
LN_EPS = 1e-5
RMS_EPS = 1e-6
LN2 = 0.6931471805599453

SAME_ENG_SYNC = True
NDMA = 24


class Sched:
    ENG = ('pe', 'act', 'dve', 'pool', 'sp')

    def __init__(self, nc, es):
        self.nc = nc
        self.e = {'pe': nc.tensor, 'act': nc.scalar, 'dve': nc.vector,
                  'pool': nc.gpsimd, 'sp': nc.sync}
        self.sem = {k: es.enter_context(nc.semaphore('sem_' + k)) for k in self.ENG}
        self.cnt = {k: 0 for k in self.ENG}
        self.seen = {k: {} for k in self.ENG}
        self.lastw = {}
        self.readers = {}
        self.dsem = [es.enter_context(nc.semaphore('dsem%d' % i)) for i in range(NDMA)]
        self.dcnt = [0] * NDMA
        self.dpool = {'pool': list(range(0, NDMA // 2)), 'sp': list(range(NDMA // 2, NDMA)),
                      'act': list(range(NDMA // 2, NDMA))}
        self.drr = {'pool': 0, 'sp': 0, 'act': 0}

    def _semobj(self, s):
        return self.dsem[s[1]] if isinstance(s, tuple) else self.sem[s]

    def _deps(self, E, reads, writes, extra=()):
        need = {}

        def add(dep):
            if dep is None:
                return
            s, c = dep
            if c > need.get(s, 0):
                need[s] = c
        for k in reads:
            add(self.lastw.get(k))
        for k in writes:
            add(self.lastw.get(k))
            for dep in self.readers.get(k, {}).items():
                add(dep)
        for dep in extra:
            add(dep)
        out = []
        for s, c in need.items():
            if s == E and (E == 'pe' or not SAME_ENG_SYNC):
                continue
            if c > self.seen[E].get(s, 0):
                self.seen[E][s] = c
                out.append((self._semobj(s), c))
        return out

    def _record(self, tag, reads, writes):
        s, c = tag
        for k in reads:
            d = self.readers.setdefault(k, {})
            if c > d.get(s, 0):
                d[s] = c
        for k in writes:
            self.lastw[k] = tag
            self.readers[k] = {}

    REGION_CFG = {0: ('prio', 0.18, 0.13, 0.18, 0.13, 0.0, 0, 0.0, 0.10, 0.215),
                  1: ('prio', 0.2, 0.15, 0.22, 0.16, 0.0, 0, 0.0, 0.125, 0.215)}

    def _cost(self, E, n):
        cfg = self.cfg
        if E == 'pe':
            big = cfg[9] if len(cfg) > 9 else 0.215
            small = cfg[8] if len(cfg) > 8 else 0.11
            return big if n >= 512 else small
        if E == 'act':
            return cfg[3] + n / 1200.0
        if E == 'dve':
            return cfg[4] + n / 960.0
        if E == 'pool':
            return 0.2 + n * 0.0027
        return 0.1

    SLACK = 0.0
    SYNC_LAT = 0.15
    ACT_FIX = 0.28
    DVE_FIX = 0.16
    CONFIGS = (('prio', 0.0), ('prio', 0.2), ('prio', 0.4), ('prio', 0.6), ('prio', 0.8), ('prio', 1.0), ('prio', 1.3), ('old', 0.5))

    def begin(self):
        self.rec = []
        self.cur = {}
        self.region = getattr(self, 'region', -1) + 1
        self.cfg = self.REGION_CFG[self.region]

    def op(self, E, fn, reads=(), writes=(), inc=True, n=None, cost=None):
        if getattr(self, 'rec', None) is not None:
            if n is None:
                n = 128 if E == 'pe' else 512
            c = cost if cost is not None else self._cost(E, n)
            u = self.cur.get(E)
            if u is None:
                u = dict(E=E, items=[], reads=[], writes=[], cost=0.0)
                self.cur[E] = u
            u['items'].append(('op', E, fn, tuple(reads), tuple(writes), inc))
            u['reads'] += list(reads)
            u['writes'] += list(writes)
            u['cost'] += c
            if inc:
                u['lat'] = u['cost']
                self.rec.append(u)
                self.cur[E] = None
            return None
        return self._emit_op(E, fn, reads, writes, inc)

    def dma(self, Q, out, in_, reads=(), writes=(), nbytes=65536, **kw):
        if getattr(self, 'rec', None) is not None:
            issue = 1.1 if Q == 'pool' else 0.15
            u = dict(E=Q, items=[('dma', Q, out, in_, tuple(reads), tuple(writes), kw)], reads=list(reads),
                     writes=list(writes), cost=issue, lat=issue + 2.0 + nbytes / 150e3)
            assert self.cur.get(Q) is None
            self.rec.append(u)
            return None
        return self._emit_dma(Q, out, in_, reads, writes, **kw)

    def end(self):
        units = self.rec
        self.rec = None
        assert all(v is None for v in self.cur.values())
        N = len(units)
        deps = [set() for _ in range(N)]
        lastw, readers = {}, {}
        for j, u in enumerate(units):
            for k in u['reads']:
                if k in lastw:
                    deps[j].add(lastw[k])
            for k in u['writes']:
                if k in lastw:
                    deps[j].add(lastw[k])
                deps[j].update(readers.get(k, ()))
            for k in u['reads']:
                readers.setdefault(k, []).append(j)
            for k in u['writes']:
                lastw[k] = j
                readers[k] = []
            deps[j].discard(j)
        succ = [[] for _ in range(N)]
        for j in range(N):
            for d in deps[j]:
                succ[d].append(j)
        prio = [0.0] * N
        for j in range(N - 1, -1, -1):
            prio[j] = units[j]['lat'] + max([prio[s] for s in succ[j]], default=0.0)
        SYNC = self.cfg[2]
        PE_MARGIN = self.cfg[7] if len(self.cfg) > 7 else 0.0
        if self.cfg[5] > 0:
            rng = np.random.RandomState(self.cfg[6])
            prio = [p * (1.0 + self.cfg[5] * rng.rand()) for p in prio]

        def simulate(mode, slack):
            indeg = [len(d) for d in deps]
            ready = [j for j in range(N) if indeg[j] == 0]
            eng_free = {}
            avail = [0.0] * N
            order = []
            while ready:
                cands = []
                for j in ready:
                    u = units[j]
                    est = eng_free.get(u['E'], 0.0)
                    for d in deps[j]:
                        t = avail[d] + (SYNC if units[d]['E'] != u['E'] else 0.05)
                        if u['E'] == 'pe' and units[d]['E'] != 'pe':
                            t += PE_MARGIN
                        if t > est:
                            est = t
                    cands.append((est, j))
                mn = min(c[0] for c in cands)
                win = [c for c in cands if c[0] <= mn + slack]
                if mode == 'old':
                    est, j = min(win, key=lambda c: (c[1], c[0]))
                else:
                    est, j = min(win, key=lambda c: (-prio[c[1]], c[0], c[1])) if slack > 0 else \
                        min(win, key=lambda c: (c[0], -prio[c[1]], c[1]))
                ready.remove(j)
                u = units[j]
                eng_free[u['E']] = est + u['cost']
                avail[j] = est + u['lat']
                order.append((est, len(order), j))
                for s in succ[j]:
                    indeg[s] -= 1
                    if indeg[s] == 0:
                        ready.append(s)
            return (max(avail) if N else 0.0), order

        best = None
        for mode, slack in (self.cfg[0:2],):
            span, order = simulate(mode, slack)
            if best is None or span < best[0]:
                best = (span, order, mode, slack)
        self.sim_span, order = best[0], best[1]
        self.sim_choice = best[2:]
        assert len(order) == N
        for _, _, j in sorted(order):
            for it in units[j]['items']:
                if it[0] == 'op':
                    _, E, fn, reads, writes, inc = it
                    self._emit_op(E, fn, reads, writes, inc)
                else:
                    _, Q, out, in_, reads, writes, kw = it
                    self._emit_dma(Q, out, in_, reads, writes, **kw)

    def _emit_op(self, E, fn, reads=(), writes=(), inc=True):
        waits = self._deps(E, reads, writes)
        eng = self.e[E]
        for (sem, c) in waits[1:]:
            eng.wait_ge(sem, c)
        inst = fn()
        if waits:
            inst._wait_ge(waits[0][0], waits[0][1])
        if inc:
            self.cnt[E] += 1
            inst.then_inc(self.sem[E], 1)
            tag = (E, self.cnt[E])
        else:
            tag = (E, self.cnt[E] + 1)
        self._record(tag, reads, writes)
        return inst

    def _emit_dma(self, Q, out, in_, reads=(), writes=(), **kw):
        pool_ids = self.dpool[Q]
        i = pool_ids[self.drr[Q] % len(pool_ids)]
        self.drr[Q] += 1
        extra = []
        if self.dcnt[i] > 0:
            extra.append((('d', i), self.dcnt[i]))
        waits = self._deps(Q, reads, writes, extra)
        eng = self.e[Q]
        for (sem, c) in waits[1:]:
            eng.wait_ge(sem, c)
        inst = eng.dma_start(out=out, in_=in_, **kw)
        if waits:
            inst._wait_ge(waits[0][0], waits[0][1])
        self.dcnt[i] += 16
        inst.then_inc(self.dsem[i], 16)
        self._record((('d', i), self.dcnt[i]), reads, writes)
        return inst

    def barrier(self):
        for E in self.ENG:
            eng = self.e[E]
            for s in self.ENG:
                if s == E:
                    continue
                c = self.cnt[s]
                if c > self.seen[E].get(s, 0):
                    self.seen[E][s] = c
                    eng.wait_ge(self.sem[s], c)
            for i in range(NDMA):
                c = self.dcnt[i]
                if c > self.seen[E].get(('d', i), 0):
                    self.seen[E][('d', i)] = c
                    eng.wait_ge(self.dsem[i], c)

    def finish(self, E='sp'):
        eng = self.e[E]
        for i in range(NDMA):
            if self.dcnt[i] > 0:
                eng.wait_ge(self.dsem[i], self.dcnt[i])
        for s in self.ENG:
            if s != E and self.cnt[s] > 0:
                eng.wait_ge(self.sem[s], self.cnt[s])


def merge(gens):
    st = [[iter(g), 0.0, float(tot), True] for g, tot in gens]
    while any(s[3] for s in st):
        s = min((s for s in st if s[3]), key=lambda s: s[1] / s[2])
        try:
            w = next(s[0])
            s[1] += (w if w else 1.0)
        except StopIteration:
            s[3] = False


def build_nc():
    nc = bass.Bass("TRN2", target_bir_lowering=False)
    dram = lambda n, sh, kind="ExternalInput": nc.dram_tensor(n, sh, F32, kind=kind).ap()
    x = dram("x", [T, D])
    w_in = dram("w_in", [D, 3328])
    w_out = dram("w_out", [D, D])
    lbl = dram("lbl", [128, 8])
    hgw = dram("hgw", [128, 4])
    snk = dram("snk", [128, 8])
    lng = dram("lng", [128, D])
    lnb = dram("lnb", [128, D])
    cosd = dram("cosd", [128, NTT * 8])
    sind = dram("sind", [128, NTT * 8])
    maskA_d = dram("maskA", [128, 512])
    mask2_d = dram("mask2", [128, 512])
    y = dram("y", [T, D], kind="ExternalOutput")

    w_in_r = w_in.rearrange("(c p) n -> p c n", p=128)
    w_out_r = w_out.rearrange("(c p) n -> p c n", p=128)

    with ExitStack() as es:
        S = Sched(nc, es)
        S.begin()

        def sb(n, sh, dt=F32, stack=es):
            return stack.enter_context(nc.sbuf_tensor(n, sh, dt))

        XT = sb("XT", [128, 8, T], BF16)
        XB = [sb("XB%d" % i, [128, D], BF16) for i in range(6)]
        WG3 = sb("WG3", [128, 8, 1280], BF16)
        WH = [sb("WH%d" % i, [128, 8, 512], BF16) for i in range(2)]
        WOUT = sb("WOUT", [128, 8, D], BF16)
        WSTG = [sb("WSTG%d" % i, [128, D], F32) for i in range(1)]
        MIXT_H = sb("MIXT_H", [128, 4, T], BF16)
        IDF = sb("IDF", [128, 128], F32)
        IDENT = sb("IDENT", [128, 128], BF16)
        CST = sb("CST", [128, 512], F32)
        MASKA = sb("MASKA", [128, 512], BF16)
        MASK2 = sb("MASK2", [128, 512], BF16)
        RESETM = sb("RESETM", [128, 1024], BF16)
        COS = sb("COS", [128, NTT, 8], F32)
        SIN = sb("SIN", [128, NTT, 8], F32)
        LNG = sb("LNG", [128, D], F32)
        LNB = sb("LNB", [128, D], F32)
        LBL = sb("LBL", [128, 8], F32)
        HGW = sb("HGW", [128, 4], F32)
        SNK = sb("SNK", [128, 8], F32)
        ESINK2 = sb("ESINK2", [128, 8], F32)
        LBD = sb("LBD", [128, 4], F32)
        TL = sb("TL", [128, 4], F32)
        ACOL = sb("ACOL", [128, 4], F32)
        BCOL = sb("BCOL", [128, 4], F32)
        CCOL = sb("CCOL", [128, 4], F32)
        NHALF = sb("NHALF", [128, 8], F32)

        BK = [es.enter_context(nc.psum_tensor("BK%d" % i, [128, 512], F32)) for i in range(8)]
        BKb = [b[:].bitcast(BF16) for b in BK]
        bk = lambda i: 'BK%d' % i

        def load_wh(h):
            hb = h % 2
            for j, col0 in enumerate((0, 512, 1024, 1536)):
                S.dma('pool', WH[hb][:, :, j * 128:(j + 1) * 128],
                      w_in_r[:, :, col0 + h * 128: col0 + (h + 1) * 128],
                      writes=['WH%d_%d' % (hb, j)], nbytes=524288)

        def load_x_tile(tt):
            b = tt % 6
            S.dma('pool', XB[b][:], x[tt * 128:(tt + 1) * 128, :], writes=['XB%d' % b], nbytes=524288)

        S.dma('sp', LBL[:], lbl[:, :], writes=['LBL'])
        S.dma('sp', HGW[:], hgw[:, :], writes=['HGW'])
        S.dma('sp', SNK[:], snk[:, :], writes=['SNK'])
        S.dma('sp', COS[:].rearrange("p a b -> p (a b)"), cosd[:, :], writes=['COS'])
        S.dma('sp', SIN[:].rearrange("p a b -> p (a b)"), sind[:, :], writes=['SIN'])
        S.dma('sp', CST[:], maskA_d[:, :], writes=['CST'])
        S.op('dve', lambda: nc.vector.tensor_copy(out=MASKA[:], in_=CST[:]), reads=['CST'], writes=['MASKA'])
        S.dma('sp', CST[:], mask2_d[:, :], writes=['CST'])
        S.op('dve', lambda: nc.vector.tensor_copy(out=MASK2[:], in_=CST[:]), reads=['CST'], writes=['MASK2'])
        S.dma('sp', LNG[:], lng[:, :], writes=['LNG'])
        S.dma('sp', LNB[:], lnb[:, :], writes=['LNB'])

        load_wh(0)

        S.op('pool', lambda: nc.gpsimd.memset(IDF[:], 1.0), writes=['IDF'])
        S.op('pool', lambda: nc.gpsimd.affine_select(out=IDF[:], in_=IDF[:], pattern=[[-1, 128]],
                                                     compare_op=ALU.is_equal, fill=0.0, base=0,
                                                     channel_multiplier=1), reads=['IDF'], writes=['IDF'])
        S.op('dve', lambda: nc.vector.tensor_copy(out=IDENT[:], in_=IDF[:]), reads=['IDF'], writes=['IDENT'])
        S.op('pool', lambda: nc.gpsimd.memset(RESETM[:], 1.0), writes=['RESETM'])
        S.op('pool', lambda: nc.gpsimd.memset(RESETM[:].rearrange("p (a b) -> p a b", b=64)[:, :, 0:1], 0.0),
             writes=['RESETM'])
        S.op('pool', lambda: nc.gpsimd.memset(NHALF[:], -0.5), writes=['NHALF'])

        S.op('dve', lambda: nc.vector.tensor_tensor(out=LBD[:], in0=LBL[:, 0:4], in1=LBL[:, 4:8], op=ALU.subtract),
             reads=['LBL'], writes=['LBD'])
        S.op('act', lambda: nc.scalar.activation(out=TL[:], in_=LBD[:], func=AF.Tanh, scale=0.5),
             reads=['LBD'], writes=['TL'])
        S.op('dve', lambda: nc.vector.tensor_scalar(out=ACOL[:], in0=TL[:], scalar1=0.25, scalar2=0.75,
                                                    op0=ALU.mult, op1=ALU.add), reads=['TL'], writes=['ACOL'])
        S.op('dve', lambda: nc.vector.tensor_scalar(out=BCOL[:], in0=TL[:], scalar1=-0.25, scalar2=0.25,
                                                    op0=ALU.mult, op1=ALU.add), reads=['TL'], writes=['BCOL'])
        S.op('dve', lambda: nc.vector.tensor_scalar(out=CCOL[:], in0=TL[:], scalar1=0.125, scalar2=-0.125,
                                                    op0=ALU.mult, op1=ALU.add), reads=['TL'], writes=['CCOL'])
        S.op('act', lambda: nc.scalar.activation(out=ESINK2[:], in_=SNK[:], func=AF.Exp, bias=LN2),
             reads=['SNK'], writes=['ESINK2'])

        evac_rr = [0]

        def evac_copy(out, in_, reads, writes):
            evac_rr[0] ^= 1
            if evac_rr[0]:
                S.op('act', lambda: nc.scalar.copy(out=out, in_=in_), reads=reads, writes=writes, n=1024)
            else:
                S.op('dve', lambda: nc.vector.tensor_copy(out=out, in_=in_), reads=reads, writes=writes, n=1024)

        def phase0_tile(tt):
            load_x_tile(tt)
            b = tt % 6
            bank = tt % 2
            pv = BKb[bank].rearrange("p (a b) -> p a b", a=8)
            for c in range(8):
                S.op('pe', lambda c=c: nc.tensor.transpose(out=pv[:, c, :], in_=XB[b][:, c * 128:(c + 1) * 128],
                                                           identity=IDENT[:]),
                     reads=['XB%d' % b, 'IDENT'], writes=[bk(bank)], inc=(c == 7))
            evac_copy(XT[:, :, tt * 128:(tt + 1) * 128], pv[:, :, :], reads=[], writes=[bk(bank), 'XT%d' % tt])

        NH = 1024
        with ExitStack() as p1:
            TH = [sb("TH%d" % i, [128, NH], F32, p1) for i in range(2)]
            THQ = sb("THQ", [128, 512], F32, p1)
            THG = sb("THG", [128, 512], F32, p1)
            Q = [sb("Q%d" % i, [128, NH], F32, p1) for i in range(2)]
            LG = sb("LG", [128, NH], F32, p1)
            G = sb("G", [128, NH], F32, p1)
            QD = [sb("QD%d" % i, [128, NH], BF16, p1) for i in range(2)]
            KDT = [sb("KDT%d" % i, [128, NH], BF16, p1) for i in range(2)]
            KDTOK = sb("KDTOK", [128, 8, 128], BF16, p1)
            VH = [sb("VH%d" % i, [128, 8, 128], BF16, p1) for i in range(3)]
            GATEH = [sb("GATEH%d" % i, [128, 8, 128], BF16, p1) for i in range(3)]
            ORAW = [sb("ORAW%d" % i, [128, 8, 128], BF16, p1) for i in range(1)]
            SBF = sb("SBF", [128, 32, 128], BF16, p1)
            R = [sb("R%d" % i, [128, 128], F32, p1) for i in range(4)]
            DEC = [sb("DEC%d" % i, [128, 32], F32, p1) for i in range(2)]
            AM = [sb("AM%d" % i, [128, 512], BF16, p1) for i in range(2)]
            SQ = [sb("SQ%d" % i, [128, 512], F32, p1) for i in range(2)]
            SS = sb("SS", [128, 8], F32, p1)
            MS = sb("MS", [128, 8], F32, p1)
            RSTD = sb("RSTD", [128, 8], F32, p1)
            MIXTOK = sb("MIXTOK", [128, 8, 128], BF16, p1)

            PROJ_BANKS = (0, 1, 7)
            proj_rr = [0]

            def next_proj_bank():
                proj_rr[0] += 1
                return PROJ_BANKS[proj_rr[0] % 3]

            def stageA(u):
                h, half = divmod(u, 2)
                hb = h % 2
                u2, u3 = u % 2, u % 3
                t0 = half * NH
                xt_keys = ['XT%d' % (half * 8 + i) for i in range(8)]
                for tp in range(4):
                    bank = next_proj_bank()
                    for j2 in range(2):
                        tt = tp * 2 + j2
                        for c in range(8):
                            S.op('pe', lambda c=c, j2=j2, tt=tt, bank=bank: nc.tensor.matmul(
                                BK[bank][:, j2 * 256:(j2 + 1) * 256],
                                lhsT=XT[:, c, t0 + tt * 128: t0 + (tt + 1) * 128],
                                rhs=WH[hb][:, c, 256:512],
                                start=(c == 0), stop=(c == 7)),
                                reads=['WH%d_2' % hb, 'WH%d_3' % hb, xt_keys[tt]], writes=[bk(bank)],
                                inc=(c == 7 and j2 == 1))
                    pv = BK[bank][:, :].rearrange("p (a b) -> p a b", a=2)
                    S.op('act', lambda tp=tp, pv=pv: nc.scalar.copy(
                        out=VH[u3][:, tp * 2:(tp + 1) * 2, :], in_=pv[:, :, 0:128]),
                        writes=[bk(bank), 'VH%d' % u3])
                    S.op('act', lambda pv=pv: nc.scalar.activation(
                        out=THG[:, 0:256].rearrange("p (a b) -> p a b", a=2), in_=pv[:, :, 128:256],
                        func=AF.Tanh, scale=0.5),
                        writes=[bk(bank), 'THG'])
                    S.op('dve', lambda tp=tp, pv=pv: nc.vector.scalar_tensor_tensor(
                        out=GATEH[u3][:, tp * 2:(tp + 1) * 2, :],
                        in0=THG[:, 0:256].rearrange("p (a b) -> p a b", a=2), scalar=1.0, in1=pv[:, :, 128:256],
                        op0=ALU.add, op1=ALU.mult),
                        reads=['THG'], writes=[bk(bank), 'GATEH%d' % u3])
                    yield 1.7
                for j in (1, 0):
                    for tg in range(2):
                        bank = next_proj_bank()
                        for c in range(8):
                            S.op('pe', lambda c=c, j=j, tg=tg, bank=bank: nc.tensor.matmul(
                                BK[bank][:, :], lhsT=WH[hb][:, c, j * 128:(j + 1) * 128],
                                rhs=XT[:, c, t0 + tg * 512: t0 + (tg + 1) * 512],
                                start=(c == 0), stop=(c == 7)),
                                reads=['WH%d_%d' % (hb, j)] + xt_keys[tg * 4:(tg + 1) * 4],
                                writes=[bk(bank)], inc=(c == 7), n=512)
                        if j == 1:
                            S.op('act', lambda tg=tg, bank=bank: nc.scalar.activation(
                                out=TH[u2][:, tg * 512:(tg + 1) * 512], in_=BK[bank][:, :], func=AF.Tanh, scale=0.5),
                                writes=[bk(bank), 'TH%d' % u2])
                        else:
                            S.op('act', lambda bank=bank: nc.scalar.activation(
                                out=THQ[:], in_=BK[bank][:, :], func=AF.Tanh, scale=0.5),
                                writes=[bk(bank), 'THQ'])
                            S.op('dve', lambda tg=tg, bank=bank: nc.vector.scalar_tensor_tensor(
                                out=Q[u2][:, tg * 512:(tg + 1) * 512], in0=THQ[:], scalar=1.0, in1=BK[bank][:, :],
                                op0=ALU.add, op1=ALU.mult),
                                reads=['THQ'], writes=[bk(bank), 'Q%d' % u2])
                        yield 1.7

            def stageB(u):
                h, half = divmod(u, 2)
                u2 = u % 2
                hp = h % 2
                S.op('act', lambda: nc.scalar.activation(out=LG[:], in_=TH[u2][:], func=AF.Ln,
                                                         scale=BCOL[:, h:h + 1], bias=ACOL[:, h:h + 1]),
                     reads=['TH%d' % u2, 'BCOL', 'ACOL'], writes=['LG'], n=1024)
                yield 1.3
                S.op('dve', lambda: nc.vector.tensor_tensor_scan(out=G[:], data0=RESETM[:], data1=LG[:], initial=0.0,
                                                                 op0=ALU.mult, op1=ALU.add),
                     reads=['RESETM', 'LG'], writes=['G'], n=2048)
                yield 2.3
                S.op('act', lambda: nc.scalar.activation(out=LG[:], in_=G[:], func=AF.Exp),
                     reads=['G'], writes=['LG'], n=1024)
                yield 1.1
                S.op('act', lambda: nc.scalar.activation(out=G[:], in_=G[:], func=AF.Exp, scale=-1.0),
                     reads=['G'], writes=['G'], n=1024)
                S.op('dve', lambda: nc.vector.tensor_copy(
                    out=DEC[hp][:, half * 16:(half + 1) * 16],
                    in_=LG[:].rearrange("p (a b) -> p a b", b=64)[:, :, 63]),
                    reads=['LG'], writes=['DEC%d' % hp], n=16)
                yield 1.1
                S.op('dve', lambda: nc.vector.scalar_tensor_tensor(out=QD[u2][:], in0=Q[u2][:], scalar=CCOL[:, h:h + 1],
                                                                   in1=LG[:], op0=ALU.mult, op1=ALU.mult),
                     reads=['Q%d' % u2, 'CCOL', 'LG'], writes=['QD%d' % u2], n=1024)
                yield 1.2
                S.op('dve', lambda: nc.vector.scalar_tensor_tensor(out=KDT[u2][:], in0=TH[u2][:], scalar=1.0,
                                                                   in1=G[:], op0=ALU.subtract, op1=ALU.mult),
                     reads=['TH%d' % u2, 'G'], writes=['KDT%d' % u2], n=1024)
                yield 1.2

            def stageC(u):
                h, half = divmod(u, 2)
                u2, u3 = u % 2, u % 3
                hp = h % 2
                kdt, qd, vh, gateh, dec = KDT[u2], QD[u2], VH[u3], GATEH[u3], DEC[hp]
                kK, kQ, kV, kG, kD = 'KDT%d' % u2, 'QD%d' % u2, 'VH%d' % u3, 'GATEH%d' % u3, 'DEC%d' % hp
                pv = BKb[2].rearrange("p (a b) -> p a b", a=8)
                for tq in range(2):
                    for tt in range(tq * 4, tq * 4 + 4):
                        S.op('pe', lambda tt=tt: nc.tensor.transpose(out=pv[:, tt, :], in_=kdt[:, tt * 128:(tt + 1) * 128],
                                                                     identity=IDENT[:]),
                             reads=[kK, 'IDENT'], writes=[bk(2)], inc=(tt % 4 == 3))
                    S.op('act', lambda tq=tq: nc.scalar.copy(out=KDTOK[:, tq * 4:(tq + 1) * 4, :],
                                                             in_=pv[:, tq * 4:(tq + 1) * 4, :]),
                         writes=[bk(2), 'KDTOK%d' % tq], n=512)
                if half == 0:
                    S.op('pool', lambda: nc.gpsimd.memset(SBF[:, 0, :], 0.0), writes=['SBF0'])
                yield 2.0
                for tq in range(2):
                    for jj in range(4):
                        tt = tq * 4 + jj
                        S.op('pe', lambda jj=jj, tt=tt: nc.tensor.matmul(
                            BK[3][:, jj * 128:(jj + 1) * 128], lhsT=kdt[:, tt * 128:(tt + 1) * 128],
                            rhs=qd[:, tt * 128:(tt + 1) * 128], start=True, stop=True),
                            reads=[kK, kQ], writes=[bk(3)], inc=(jj == 3))
                    ab = tq % 2
                    S.op('dve', lambda ab=ab: nc.vector.tensor_tensor(out=AM[ab][:], in0=BK[3][:, :], in1=MASKA[:],
                                                                      op=ALU.mult),
                         reads=['MASKA'], writes=[bk(3), 'AM%d' % ab])
                    for jj in range(4):
                        tt = tq * 4 + jj
                        for cj in range(2):
                            S.op('pe', lambda tt=tt, cj=cj, jj=jj: nc.tensor.matmul(
                                BK[4 + cj][:, jj * 128:(jj + 1) * 128],
                                lhsT=KDTOK[cj * 64:(cj + 1) * 64, tt, :],
                                rhs=vh[cj * 64:(cj + 1) * 64, tt, :], start=True, stop=True),
                                reads=['KDTOK%d' % tq, kV], writes=[bk(4 + cj)],
                                inc=(jj == 3))
                    yield 1.5
                    for jj in range(4):
                        for cj in range(2):
                            n = half * 16 + tq * 8 + jj * 2 + cj
                            ubank = 4 + cj
                            col = jj * 128
                            rb = n % 4
                            rp = (n - 1) % 4
                            if n == 0:
                                S.op('dve', lambda col=col, ubank=ubank: nc.vector.tensor_copy(
                                    out=R[0][:], in_=BK[ubank][:, col:col + 128]),
                                    writes=[bk(ubank), 'R0'], n=128, cost=0.36)
                            else:
                                S.op('dve', lambda col=col, ubank=ubank, rb=rb, rp=rp, n=n: nc.vector.scalar_tensor_tensor(
                                    out=R[rb][:], in0=R[rp][:], scalar=dec[:, n - 1:n],
                                    in1=BK[ubank][:, col:col + 128], op0=ALU.mult, op1=ALU.add),
                                    reads=['R%d' % rp, kD], writes=[bk(ubank), 'R%d' % rb], n=128, cost=0.36)
                            if n < 31:
                                S.op('pool', lambda rb=rb, n=n: nc.gpsimd.tensor_scalar(
                                    out=SBF[:, n + 1, :], in0=R[rb][:], scalar1=dec[:, n:n + 1], scalar2=1.0,
                                    op0=ALU.mult, op1=ALU.mult),
                                    reads=['R%d' % rb, kD], writes=['SBF%d' % (n + 1)], n=128, cost=0.4)
                        yield 0.9
                    obank = 6
                    for jj in range(4):
                        tt = tq * 4 + jj
                        S.op('pe', lambda jj=jj, tt=tt, ab=ab, obank=obank: nc.tensor.matmul(
                            BK[obank][:, jj * 128:(jj + 1) * 128], lhsT=AM[ab][:, jj * 128:(jj + 1) * 128],
                            rhs=vh[:, tt, :], start=True, stop=False, skip_group_check=True),
                            reads=['AM%d' % ab, kV], writes=[bk(obank)], inc=False)
                        for cj in range(2):
                            n = half * 16 + tt * 2 + cj
                            S.op('pe', lambda jj=jj, tt=tt, cj=cj, n=n, obank=obank: nc.tensor.matmul(
                                BK[obank][cj * 64:(cj + 1) * 64, jj * 128:(jj + 1) * 128],
                                lhsT=qd[:, tt * 128 + cj * 64: tt * 128 + (cj + 1) * 64],
                                rhs=SBF[:, n, :], start=False, stop=(cj == 1), skip_group_check=True),
                                reads=[kQ, 'SBF%d' % n], writes=[bk(obank)],
                                inc=(jj == 3 and cj == 1))
                    pvo = BK[obank][:, :].rearrange("p (a b) -> p a b", a=4)
                    S.op('act', lambda obank=obank, tq=tq: nc.scalar.activation(out=SQ[tq][:], in_=BK[obank][:, :],
                                                                                func=AF.Square),
                         writes=[bk(obank), 'SQ%d' % tq])
                    S.op('act', lambda tq=tq, pvo=pvo: nc.scalar.copy(out=ORAW[0][:, tq * 4:(tq + 1) * 4, :], in_=pvo),
                         writes=[bk(obank), 'ORAW%d' % tq])
                    S.op('dve', lambda tq=tq: nc.vector.tensor_reduce(
                        out=SS[:, tq * 4:(tq + 1) * 4], in_=SQ[tq][:].rearrange("p (a b) -> p a b", a=4),
                        axis=AX.X, op=ALU.add), reads=['SQ%d' % tq], writes=['SS%d' % tq])
                    yield 2.0
                    S.op('dve', lambda tq=tq: nc.vector.tensor_scalar(
                        out=MS[:, tq * 4:(tq + 1) * 4], in0=SS[:, tq * 4:(tq + 1) * 4], scalar1=1.0 / 128.0,
                        scalar2=RMS_EPS, op0=ALU.mult, op1=ALU.add), reads=['SS%d' % tq], writes=['MS%d' % tq], n=4)
                    S.op('pool', lambda tq=tq: nc.gpsimd.tensor_tensor(
                        out=RSTD[:, tq * 4:(tq + 1) * 4], in0=MS[:, tq * 4:(tq + 1) * 4], in1=NHALF[:, 0:4],
                        op=ALU.pow), reads=['MS%d' % tq, 'NHALF'], writes=['RSTD%d' % tq], cost=1.0)
                    for tt in range(tq * 4, tq * 4 + 4):
                        S.op('dve', lambda tt=tt: nc.vector.scalar_tensor_tensor(
                            out=MIXTOK[:, tt, :], in0=ORAW[0][:, tt, :], scalar=RSTD[:, tt:tt + 1],
                            in1=gateh[:, tt, :], op0=ALU.mult, op1=ALU.mult),
                            reads=['ORAW%d' % tq, 'RSTD%d' % tq, kG], writes=['MIXTOK%d' % tq], n=128, cost=0.36)
                    for tt in range(tq * 4, tq * 4 + 4):
                        S.op('pe', lambda tt=tt: nc.tensor.transpose(out=pv[:, tt, :], in_=MIXTOK[:, tt, :],
                                                                     identity=IDENT[:]),
                             reads=['MIXTOK%d' % tq, 'IDENT'], writes=[bk(2)], inc=(tt % 4 == 3))
                    S.op('act', lambda tq=tq: nc.scalar.copy(
                        out=MIXT_H[:, h, half * NH + tq * 512: half * NH + (tq + 1) * 512].rearrange(
                            "p (a b) -> p a b", a=4),
                        in_=pv[:, tq * 4:(tq + 1) * 4, :]),
                        writes=[bk(2)] + ['MIXTH%d' % (half * 8 + tq * 4 + i) for i in range(4)], n=512)
                    yield 2.0

            def prefetch_rest():
                S.dma('pool', WG3[:, :, 0:512], w_in_r[:, :, 2048:2560], writes=['WG3q'], nbytes=2097152)
                S.dma('pool', WG3[:, :, 512:768], w_in_r[:, :, 2560:2816], writes=['WG3kv'], nbytes=1048576)
                S.dma('pool', WG3[:, :, 768:1280], w_in_r[:, :, 2816:3328], writes=['WG3g'], nbytes=2097152)
                S.dma('pool', WOUT[:, 4:8, :], w_out_r[:, 4:8, :], writes=['WOUT_A'], nbytes=2097152)
                for c in range(4):
                    b = 0
                    S.dma('sp', WSTG[b][:], w_out[c * 128:(c + 1) * 128, :], writes=['WSTG%d' % b], nbytes=524288)
                    S.op('pool', lambda c=c, b=b: nc.gpsimd.tensor_scalar(
                        out=WOUT[:, c, :], in0=WSTG[b][:], scalar1=HGW[:, c:c + 1], scalar2=0.5,
                        op0=ALU.mult, op1=ALU.mult), reads=['WSTG%d' % b, 'HGW'], writes=['WOUT_H%d' % c], n=1024)

            for tt in range(8):
                phase0_tile(tt)
            load_wh(1)
            for tt in range(8, 16):
                phase0_tile(tt)
            for u in range(8):
                h, half = divmod(u, 2)
                if half == 0 and h >= 2:
                    load_wh(h)
                for _ in stageA(u):
                    pass
                for _ in stageB(u):
                    pass
                for _ in stageC(u):
                    pass
                if u == 3:
                    prefetch_rest()
            S.end()
            S.barrier()

        with ExitStack() as p2:
            RAW = [sb("RAW%d" % i, [128, 12, 64], F32, p2) for i in range(2)]
            QKTOK = [sb("QKTOK%d" % i, [128, 12, 64], BF16, p2) for i in range(2)]
            ROPT = [sb("ROPT%d" % i, [128, 12, 8], F32, p2) for i in range(4)]
            QT = [sb("QT%d" % i, [128, 4, 128], BF16, p2) for i in range(2)]
            KT2 = sb("KT2", [128, 3, 2, 128], BF16, p2)
            VAUG = sb("VAUG", [128, 3, 2, 66], BF16, p2)
            THA = [sb("THA%d" % i, [128, 512], F32, p2) for i in range(2)]
            AG = [sb("AG%d" % i, [128, 512], F32, p2) for i in range(2)]
            OZ = [sb("OZ%d" % i, [128, D], F32, p2) for i in range(2)]
            GATEA = [sb("GATEA%d" % i, [128, 512], BF16, p2) for i in range(2)]
            PT = [sb("PT%d" % i, [128, 512], BF16, p2) for i in range(4)]
            DEN = [sb("DEN%d" % i, [128, 4], F32, p2) for i in range(2)]
            REC = [sb("REC%d" % i, [128, 4], F32, p2) for i in range(2)]
            TMPN = [sb("TMPN%d" % i, [128, 4, 64], F32, p2) for i in range(2)]
            MIXA = [sb("MIXA%d" % i, [128, 512], BF16, p2) for i in range(2)]
            MIXTA = [sb("MIXTA%d" % i, [128, 4, 128], BF16, p2) for i in range(2)]
            XRES = [sb("XRES%d" % i, [128, D], F32, p2) for i in range(2)]
            ZN = [sb("ZN%d" % i, [128, D], F32, p2) for i in range(2)]
            STATS = sb("STATS", [128, 2, 6], F32, p2)
            MV = [sb("MV%d" % i, [128, 2], F32, p2) for i in range(2)]
            VE = sb("VE", [128, 1], F32, p2)
            RS = [sb("RS%d" % i, [128, 1], F32, p2) for i in range(2)]

            S.begin()
            S.op('pool', lambda: nc.gpsimd.memset(VAUG[:, :, :, 64:65], 1.0), writes=['VAUG0', 'VAUG1', 'VAUG2'])
            pt_rr = [0]
            SB2 = (5, 6)
            PVB = (3, 4)
            OPB = (1, 2)
            TB = 7

            def stageA1(n):
                b = n % 2
                slot = n % 3
                S.dma('sp', XRES[b][:], x[n * 128:(n + 1) * 128, :], writes=['XRES%d' % b], nbytes=524288)
                def proj(c0, c1, key):
                    for c in range(8):
                        S.op('pe', lambda c=c, c0=c0, c1=c1: nc.tensor.matmul(
                            BK[0][:, 0:c1 - c0], lhsT=XT[:, c, n * 128:(n + 1) * 128], rhs=WG3[:, c, c0:c1],
                            start=(c == 0), stop=(c == 7)),
                            reads=[key, 'XT%d' % n], writes=[bk(0)], inc=(c == 7), n=c1 - c0)
                proj(0, 512, 'WG3q')
                S.op('act', lambda: nc.scalar.copy(out=RAW[b][:, 0:8, :],
                                                   in_=BK[0][:, :].rearrange("p (a b) -> p a b", a=8)),
                     writes=[bk(0), 'RAW%d' % b])
                proj(512, 768, 'WG3kv')
                S.op('act', lambda: nc.scalar.copy(
                    out=RAW[b][:, 8:12, :].rearrange("p (g r) d -> p g r d", g=2),
                    in_=BK[0][:, 0:128].rearrange("p (g d) -> p g d", g=2).unsqueeze(2).to_broadcast([128, 2, 2, 64])),
                    writes=[bk(0), 'RAW%d' % b])
                S.op('act', lambda: nc.scalar.copy(out=VAUG[:, slot, :, 0:64],
                                                   in_=BK[0][:, 128:256].rearrange("p (g d) -> p g d", g=2)),
                     writes=[bk(0), 'VAUG%d' % slot], n=128)
                proj(768, 1280, 'WG3g')
                S.op('act', lambda: nc.scalar.activation(out=THA[b][:], in_=BK[0][:, :], func=AF.Tanh, scale=0.5),
                     writes=[bk(0), 'THA%d' % b])
                S.op('act', lambda: nc.scalar.copy(out=AG[b][:], in_=BK[0][:, :]), writes=[bk(0), 'AG%d' % b])
                S.op('dve', lambda: nc.vector.scalar_tensor_tensor(out=GATEA[b][:], in0=THA[b][:], scalar=1.0,
                                                                   in1=AG[b][:], op0=ALU.add, op1=ALU.mult),
                     reads=['THA%d' % b, 'AG%d' % b], writes=['GATEA%d' % b])
                S.op('act', lambda: nc.scalar.copy(out=QKTOK[b][:, :, 16:64], in_=RAW[b][:, :, 16:64]),
                     reads=['RAW%d' % b], writes=['QKTOK%d' % b])
                cosb = COS[:, n, :].unsqueeze(1).to_broadcast([128, 12, 8])
                sinb = SIN[:, n, :].unsqueeze(1).to_broadcast([128, 12, 8])
                x1 = RAW[b][:, :, 0:8]
                x2 = RAW[b][:, :, 8:16]
                S.op('pool', lambda: nc.gpsimd.tensor_tensor(out=ROPT[0][:], in0=x1, in1=cosb, op=ALU.mult),
                     reads=['RAW%d' % b, 'COS'], writes=['ROPT0'], n=96)
                S.op('pool', lambda: nc.gpsimd.tensor_tensor(out=ROPT[1][:], in0=x2, in1=sinb, op=ALU.mult),
                     reads=['RAW%d' % b, 'SIN'], writes=['ROPT1'], n=96)
                S.op('pool', lambda: nc.gpsimd.tensor_tensor(out=QKTOK[b][:, :, 0:8], in0=ROPT[0][:], in1=ROPT[1][:],
                                                            op=ALU.subtract),
                     reads=['ROPT0', 'ROPT1'], writes=['QKTOK%d' % b], n=96)
                S.op('pool', lambda: nc.gpsimd.tensor_tensor(out=ROPT[2][:], in0=x2, in1=cosb, op=ALU.mult),
                     reads=['RAW%d' % b, 'COS'], writes=['ROPT2'], n=96)
                S.op('pool', lambda: nc.gpsimd.tensor_tensor(out=ROPT[3][:], in0=x1, in1=sinb, op=ALU.mult),
                     reads=['RAW%d' % b, 'SIN'], writes=['ROPT3'], n=96)
                S.op('pool', lambda: nc.gpsimd.tensor_tensor(out=QKTOK[b][:, :, 8:16], in0=ROPT[2][:], in1=ROPT[3][:],
                                                            op=ALU.add),
                     reads=['ROPT2', 'ROPT3'], writes=['QKTOK%d' % b], n=96)

            def stageA2(n):
                b = n % 2
                slot = n % 3
                pvq = BKb[TB].rearrange("p (a b) -> p a b", a=8)
                qk2 = QKTOK[b][:].rearrange("p (a r) d -> p a (r d)", r=2)
                for i in range(6):
                    S.op('pe', lambda i=i: nc.tensor.transpose(out=pvq[:, i, :], in_=qk2[:, i, :], identity=IDENT[:]),
                         reads=['QKTOK%d' % b, 'IDENT'], writes=[bk(TB)], inc=(i == 5))
                S.op('act', lambda: nc.scalar.copy(out=QT[b][:, :, :], in_=pvq[:, 0:4, :]),
                     writes=[bk(TB), 'QT%d' % b])
                S.op('act', lambda: nc.scalar.copy(out=KT2[:, slot, :, :], in_=pvq[:, 4:6, :]),
                     writes=[bk(TB), 'KT2_%d' % slot], n=256)

            def stageB1(n):
                b = n % 2
                slot = n % 3
                pslot = (n - 1) % 3
                kts = ([(0, pslot)] if n > 0 else []) + [(1, slot)]
                pis = []
                for g in range(2):
                    pi2 = []
                    for uu in range(2):
                        pi2.append(pt_rr[0] % 4)
                        pt_rr[0] += 1
                    pis.append(pi2)
                    for idx, (kt, ks) in enumerate(kts):
                        for uu in range(2):
                            S.op('pe', lambda uu=uu, ks=ks, kt=kt, g=g: nc.tensor.matmul(
                                BK[SB2[uu]][:, kt * 256:(kt + 1) * 256],
                                lhsT=KT2[uu * 64:(uu + 1) * 64, ks, g, :],
                                rhs=QT[b][uu * 64:(uu + 1) * 64, 2 * g:2 * g + 2, :], start=True, stop=True),
                                reads=['KT2_%d' % ks, 'QT%d' % b], writes=[bk(SB2[uu])],
                                inc=(idx == len(kts) - 1), n=256)
                    for uu in range(2):
                        if n > 0:
                            sel = lambda ap: ap
                        else:
                            sel = lambda ap: ap[:, 256:512]
                        S.op('act', lambda uu=uu, pi2=pi2, sel=sel: nc.scalar.activation(
                            out=sel(PT[pi2[uu]][:]), in_=sel(BK[SB2[uu]][:, :]), func=AF.Exp, scale=0.125),
                            writes=[bk(SB2[uu]), 'PT%d' % pi2[uu]])
                        S.op('dve', lambda uu=uu, pi2=pi2, sel=sel: nc.vector.tensor_tensor(
                            out=sel(PT[pi2[uu]][:]), in0=sel(PT[pi2[uu]][:]), in1=sel(MASK2[:]), op=ALU.mult),
                            reads=['MASK2'], writes=['PT%d' % pi2[uu]], cost=0.45)
                for g in range(2):
                    pi2 = pis[g]
                    for pl in range(2):
                        p = g * 2 + pl
                        for uu in range(2):
                            hh = 2 * p + uu
                            vbank = PVB[hh // 4]
                            col = (hh % 4) * 66
                            for idx, (kt, ks) in enumerate(kts):
                                sl = kt * 2 + pl
                                S.op('pe', lambda sl=sl, ks=ks, uu=uu, vbank=vbank, col=col, idx=idx, g=g, pi2=pi2:
                                     nc.tensor.matmul(
                                         BK[vbank][:, col:col + 65], lhsT=PT[pi2[uu]][:, sl * 128:(sl + 1) * 128],
                                         rhs=VAUG[:, ks, g, 0:65], start=(idx == 0), stop=(idx == len(kts) - 1)),
                                     reads=['PT%d' % pi2[uu], 'VAUG%d' % ks], writes=[bk(vbank)],
                                     inc=(idx == len(kts) - 1))
                for vb in range(2):
                    vbank = PVB[vb]
                    pvv = BK[vbank][:, 0:264].rearrange("p (a b) -> p a b", a=4)
                    S.op('dve', lambda vb=vb, pvv=pvv: nc.vector.scalar_tensor_tensor(
                        out=DEN[vb][:], in0=pvv[:, :, 64], scalar=2.0,
                        in1=ESINK2[:, vb * 4:(vb + 1) * 4], op0=ALU.mult, op1=ALU.add),
                        reads=['ESINK2'], writes=[bk(vbank), 'DEN%d' % vb], n=4)
                    S.op('dve', lambda vb=vb: nc.vector.reciprocal(out=REC[vb][:], in_=DEN[vb][:]),
                         reads=['DEN%d' % vb], writes=['REC%d' % vb], n=32)
                    S.op('dve', lambda vb=vb, pvv=pvv: nc.vector.tensor_tensor(
                        out=TMPN[vb][:], in0=pvv[:, :, 0:64],
                        in1=REC[vb][:].unsqueeze(2).to_broadcast([128, 4, 64]), op=ALU.mult),
                        reads=['REC%d' % vb], writes=[bk(vbank), 'TMPN%d' % vb], n=256)
                    S.op('dve', lambda vb=vb: nc.vector.tensor_tensor(
                        out=MIXA[b][:, vb * 256:(vb + 1) * 256].rearrange("p (a b) -> p a b", a=4),
                        in0=TMPN[vb][:],
                        in1=GATEA[b][:, vb * 256:(vb + 1) * 256].rearrange("p (a b) -> p a b", a=4), op=ALU.mult),
                        reads=['TMPN%d' % vb, 'GATEA%d' % b], writes=['MIXA%d' % b], n=256)

            def stageB2(n):
                b = n % 2
                pvm = BKb[TB].rearrange("p (a b) -> p a b", a=8)
                for c in range(4):
                    S.op('pe', lambda c=c: nc.tensor.transpose(out=pvm[:, c, :], in_=MIXA[b][:, c * 128:(c + 1) * 128],
                                                               identity=IDENT[:]),
                         reads=['MIXA%d' % b, 'IDENT'], writes=[bk(TB)], inc=(c == 3))
                S.op('act', lambda: nc.scalar.copy(out=MIXTA[b][:, :, :], in_=pvm[:, 0:4, :]),
                     writes=[bk(TB), 'MIXTA%d' % b])
                for hf in range(2):
                    zb = OPB[hf]
                    for c in range(8):
                        if c < 4:
                            lhsT = MIXT_H[:, c, n * 128:(n + 1) * 128]
                            rk = ['MIXTH%d' % n, 'WOUT_H%d' % c]
                        else:
                            lhsT = MIXTA[b][:, c - 4, :]
                            rk = ['MIXTA%d' % b, 'WOUT_A']
                        S.op('pe', lambda c=c, lhsT=lhsT, zb=zb, hf=hf: nc.tensor.matmul(
                            BK[zb][:, :], lhsT=lhsT, rhs=WOUT[:, c, hf * 512:(hf + 1) * 512],
                            start=(c == 0), stop=(c == 7)),
                            reads=rk, writes=[bk(zb)], inc=(c == 7), n=512)
                    S.op('act', lambda hf=hf, zb=zb: nc.scalar.copy(out=OZ[b][:, hf * 512:(hf + 1) * 512],
                                                                    in_=BK[zb][:, :]),
                         writes=[bk(zb), 'OZ%d' % b])

            def stageB3a(n):
                b = n % 2
                for hf in range(2):
                    zb = PVB[hf]
                    S.op('dve', lambda hf=hf, zb=zb: nc.vector.scalar_tensor_tensor(
                        out=XRES[b][:, hf * 512:(hf + 1) * 512], in0=XRES[b][:, hf * 512:(hf + 1) * 512],
                        scalar=DN_ALPHA, in1=OZ[b][:, hf * 512:(hf + 1) * 512], op0=ALU.mult, op1=ALU.add),
                        reads=['OZ%d' % b], writes=['XRES%d' % b])
                for hf in range(2):
                    S.op('dve', lambda hf=hf: nc.vector.bn_stats(out=STATS[:, hf, :],
                                                                 in_=XRES[b][:, hf * 512:(hf + 1) * 512]),
                         reads=['XRES%d' % b], writes=['STATS'])
                S.op('dve', lambda: nc.vector.bn_aggr(out=MV[b][:], in_=STATS[:].rearrange("p a b -> p (a b)")),
                     reads=['STATS'], writes=['MV%d' % b], n=12)
                S.op('dve', lambda: nc.vector.tensor_scalar(out=VE[:], in0=MV[b][:, 1:2], scalar1=LN_EPS, scalar2=None,
                                                            op0=ALU.add), reads=['MV%d' % b], writes=['VE'], n=1)
                S.op('pool', lambda: nc.gpsimd.tensor_tensor(out=RS[b][:], in0=VE[:], in1=NHALF[:, 0:1], op=ALU.pow),
                     reads=['VE', 'NHALF'], writes=['RS%d' % b], cost=0.55)

            def stageB3b(n):
                b = n % 2
                S.op('dve', lambda: nc.vector.scalar_tensor_tensor(
                    out=XRES[b][:], in0=XRES[b][:], scalar=MV[b][:, 0:1], in1=LNG[:],
                    op0=ALU.subtract, op1=ALU.mult),
                    reads=['MV%d' % b, 'LNG'], writes=['XRES%d' % b], n=1024)
                S.op('dve', lambda: nc.vector.scalar_tensor_tensor(
                    out=ZN[b][:], in0=XRES[b][:], scalar=RS[b][:, 0:1], in1=LNB[:],
                    op0=ALU.mult, op1=ALU.add),
                    reads=['XRES%d' % b, 'RS%d' % b, 'LNB'], writes=['ZN%d' % b], n=1024)
                S.dma('sp', y[n * 128:(n + 1) * 128, :], ZN[b][:], reads=['ZN%d' % b], nbytes=524288)

            for i in range(NTT):
                stageA1(i)
                stageA2(i)
                stageB1(i)
                stageB2(i)
                stageB3a(i)
                stageB3b(i)
            S.end()
            S.finish('sp')
    return nc


_NC_CACHE = {}


def _consts():
    ROPE_THETA = 500000.0
    ROPE_DIM = 16
    pos = np.arange(T, dtype=np.float32)
    inv_freq = (ROPE_THETA ** (-np.arange(0, ROPE_DIM, 2, dtype=np.float32) / ROPE_DIM)).astype(np.float32)
    ang = (pos[:, None] * inv_freq[None, :]).astype(np.float32)
    cos = np.cos(ang).astype(np.float32)
    sin = np.sin(ang).astype(np.float32)
    lay = lambda a: np.ascontiguousarray(a.reshape(NTT, 128, 8).transpose(1, 0, 2).reshape(128, NTT * 8))
    s = np.arange(128)[:, None]
    t = np.arange(128)[None, :]
    mA = ((s // 64 == t // 64) & (s <= t)).astype(np.float32)
    maskA = np.tile(mA, (1, 4))
    mprev = (s > t).astype(np.float32)
    mcur = (s <= t).astype(np.float32)
    mask2 = np.concatenate([mprev, mprev, mcur, mcur], axis=1)
    return lay(cos), lay(sin), np.ascontiguousarray(maskA), np.ascontiguousarray(mask2)


def kernel(x, w_in, lb_logits, hg_norm_w, sinks, w_out, ln_g, ln_b):
    x = np.asarray(x, dtype=np.float32)
    B = x.shape[0]
    if 'nc' not in _NC_CACHE:
        _NC_CACHE['nc'] = build_nc()
    nc = _NC_CACHE['nc']
    cosd, sind, maskA, mask2 = _consts()
    w_in0 = np.ascontiguousarray(np.asarray(w_in, np.float32)[0])
    w_out0 = np.ascontiguousarray(np.asarray(w_out, np.float32)[0])
    lbl = np.ascontiguousarray(np.asarray(lb_logits, np.float32).reshape(2, 4, 128).transpose(2, 0, 1).reshape(128, 8))
    hgw = np.ascontiguousarray(np.asarray(hg_norm_w, np.float32).reshape(4, 128).T)
    snk = np.ascontiguousarray(np.broadcast_to(np.asarray(sinks, np.float32).reshape(1, 8), (128, 8)))
    lng = np.ascontiguousarray(np.broadcast_to(np.asarray(ln_g, np.float32).reshape(1, D), (128, D)))
    lnb = np.ascontiguousarray(np.broadcast_to(np.asarray(ln_b, np.float32).reshape(1, D), (128, D)))
    common = dict(w_in=w_in0, w_out=w_out0, lbl=lbl, hgw=hgw, snk=snk, lng=lng, lnb=lnb,
                  cosd=cosd, sind=sind, maskA=maskA, mask2=mask2)
    in_maps = [dict(common, x=np.ascontiguousarray(x[b])) for b in range(B)]
    res = run_bass_kernel_spmd(nc, in_maps, core_ids=list(range(B)))
    return np.stack([np.asarray(r["y"], dtype=np.float32) for r in res.results], axis=0)
```
